# Optimizing a Trainium2 kernel written in Bass

```python
import math
import jax, jax.numpy as jnp
from jax import lax
import numpy as np

D_MODEL = 1024
BATCH = 8
SEQ = 2048
DEPTH = 1

HEAD_DIM = 64
SWA_HEADS = 8
SWA_KV_HEADS = 2
SWA_WINDOW = 128
MOBA_HEADS = 8
MOBA_KV_HEADS = 2
MOBA_BLOCK = 256
MOBA_TOPK = 3
Q_CHUNK = 128
D_FF = int(math.ceil(8 * D_MODEL / 3 / 128)) * 128
ALPHA = (2.0 * DEPTH) ** 0.25
BETA = (8.0 * DEPTH) ** -0.25
LN_EPS = 1e-5
NEG = -1e30

SWA_Q = SWA_HEADS * HEAD_DIM
SWA_KV = SWA_KV_HEADS * HEAD_DIM
MOBA_Q = MOBA_HEADS * HEAD_DIM
MOBA_KV = MOBA_KV_HEADS * HEAD_DIM
IN_SPLITS = (SWA_Q, SWA_KV, SWA_KV, MOBA_Q, MOBA_KV, MOBA_KV, D_MODEL, D_MODEL)
IN_COLS = sum(IN_SPLITS)
N_ALIBI = SWA_HEADS + MOBA_HEADS

kernel_name = "hybrid_swa_sink_moba_macaron_deepnorm"


def alibi_slopes():
    i = jnp.arange(1, N_ALIBI + 1, dtype=jnp.float32)
    return jnp.exp2(-8.0 * i / N_ALIBI)


def layer_norm(x, g, b):
    xf = x.astype(jnp.float32)
    mu = jnp.mean(xf, axis=-1, keepdims=True)
    var = jnp.mean(jnp.square(xf - mu), axis=-1, keepdims=True)
    y = (xf - mu) * lax.rsqrt(var + LN_EPS) * g.astype(jnp.float32) + b.astype(jnp.float32)
    return y.astype(x.dtype)


def swiglu_ffn(x, w_in, w_out):
    a, u = jnp.split(x @ w_in, 2, axis=-1)
    return (jax.nn.silu(a) * u) @ w_out


def swa_sink_attention(q, k, v, sinks, slopes):
    B, S, H, d = q.shape
    Hkv = k.shape[2]
    G = H // Hkv
    W = SWA_WINDOW
    nb = S // W
    qb = q.reshape(B, nb, W, Hkv, G, d)
    kb = k.reshape(B, nb, W, Hkv, d)
    vb = v.reshape(B, nb, W, Hkv, d)
    pad = ((0, 0), (1, 0), (0, 0), (0, 0), (0, 0))
    kwin = jnp.concatenate([jnp.pad(kb, pad)[:, :nb], kb], axis=2)
    vwin = jnp.concatenate([jnp.pad(vb, pad)[:, :nb], vb], axis=2)
    s = jnp.einsum('bnqkgd,bnskd->bnkgqs', qb, kwin).astype(jnp.float32) * (1.0 / math.sqrt(d))
    qi = jnp.arange(W)[:, None]
    kj = jnp.arange(2 * W)[None, :]
    dist = qi + W - kj
    kpos = jnp.arange(nb)[:, None, None] * W - W + kj[None]
    allowed = (dist >= 0) & (dist < W) & (kpos >= 0)
    sl = slopes.reshape(Hkv, G)
    s = s - sl[:, :, None, None] * dist.astype(jnp.float32)
    s = jnp.where(allowed[None, :, None, None], s, NEG)
    sink = sinks.astype(jnp.float32).reshape(Hkv, G)[None, None, :, :, None, None]
    m = jnp.maximum(jnp.max(s, axis=-1, keepdims=True), sink)
    p = jnp.exp(s - m)
    p = p / (jnp.sum(p, axis=-1, keepdims=True) + jnp.exp(sink - m))
    o = jnp.einsum('bnkgqs,bnskd->bnqkgd', p.astype(v.dtype), vwin)
    return o.reshape(B, S, H, d)


def moba_attention(q, k, v, slopes):
    B, S, H, d = q.shape
    Hkv = k.shape[2]
    G = H // Hkv
    BL = MOBA_BLOCK
    C = Q_CHUNK
    Sp = -(-S // BL) * BL
    if Sp != S:
        pw = ((0, 0), (0, Sp - S), (0, 0), (0, 0))
        q, k, v = jnp.pad(q, pw), jnp.pad(k, pw), jnp.pad(v, pw)
    nb = Sp // BL
    nc = Sp // C
    scale = 1.0 / math.sqrt(d)
    kb = k.reshape(B, nb, BL, Hkv, d)
    vb = v.reshape(B, nb, BL, Hkv, d)
    n_sel = min(MOBA_TOPK, nb - 1)
    q_blk = jnp.arange(Sp) // BL
    if n_sel > 0:
        kmean = jnp.mean(kb.astype(jnp.float32), axis=2)
        gate = jnp.einsum('bskgd,bnkd->bskgn', q.reshape(B, Sp, Hkv, G, d).astype(jnp.float32),
                          kmean).reshape(B, Sp, H, nb)
        past = jnp.arange(nb)[None, :] < q_blk[:, None]
        gate = jnp.where(past[None, :, None, :], gate, -jnp.inf)
        _, idx = lax.top_k(gate, n_sel)
        valid = idx < q_blk[None, :, None, None]
    else:
        idx = jnp.zeros((B, Sp, H, 0), jnp.int32)
        valid = jnp.zeros((B, Sp, H, 0), bool)
    kh = jnp.repeat(kb, G, axis=3).transpose(0, 1, 3, 2, 4)
    vh = jnp.repeat(vb, G, axis=3).transpose(0, 1, 3, 2, 4)
    bidx = jnp.arange(B)[:, None, None, None]
    hidx = jnp.arange(H)[None, None, :, None]

    def to_chunks(a):
        return a.reshape((B, nc, C) + a.shape[2:]).transpose((1, 0, 2) + tuple(range(3, a.ndim + 1)))

    def one_chunk(xs):
        c, qc, idxc, validc = xs
        pos_q = c * C + jnp.arange(C)
        own = (c * C) // BL
        k_own = lax.dynamic_index_in_dim(kb, own, axis=1, keepdims=False)
        v_own = lax.dynamic_index_in_dim(vb, own, axis=1, keepdims=False)
        pos_own = own * BL + jnp.arange(BL)
        s_own = jnp.einsum('bqkgd,bskd->bqkgs', qc.reshape(B, C, Hkv, G, d), k_own)
        s_own = s_own.astype(jnp.float32).reshape(B, C, H, BL) * scale
        dist_own = (pos_q[:, None] - pos_own[None, :]).astype(jnp.float32)
        s_own = s_own - slopes[None, None, :, None] * dist_own[None, :, None, :]
        s_own = jnp.where((dist_own >= 0)[None, :, None, :], s_own, NEG)
        k_sel = kh[bidx, idxc, hidx]
        v_sel = vh[bidx, idxc, hidx]
        s_sel = jnp.einsum('bqhd,bqhnsd->bqhns', qc, k_sel).astype(jnp.float32) * scale
        pos_sel = idxc[..., None] * BL + jnp.arange(BL)
        dist_sel = (pos_q[None, :, None, None, None] - pos_sel).astype(jnp.float32)
        s_sel = s_sel - slopes[None, None, :, None, None] * dist_sel
        s_sel = jnp.where(validc[..., None], s_sel, NEG).reshape(B, C, H, n_sel * BL)
        p = jax.nn.softmax(jnp.concatenate([s_sel, s_own], axis=-1), axis=-1)
        p_sel = p[..., :n_sel * BL].reshape(B, C, H, n_sel, BL).astype(qc.dtype)
        p_own = p[..., n_sel * BL:].reshape(B, C, Hkv, G, BL).astype(qc.dtype)
        o = jnp.einsum('bqhns,bqhnsd->bqhd', p_sel, v_sel)
        o = o + jnp.einsum('bqkgs,bskd->bqkgd', p_own, v_own).reshape(B, C, H, d)
        return o

    xs = (jnp.arange(nc, dtype=jnp.int32), to_chunks(q), to_chunks(idx), to_chunks(valid))
    out = lax.map(one_chunk, xs)
    return out.transpose(1, 0, 2, 3, 4).reshape(B, Sp, H, d)[:, :S]


def hybrid_mixer(x, w_in, sinks, w_branch_a, w_branch_b, w_o):
    B, S, _ = x.shape
    h = x @ w_in
    offs = list(np.cumsum(IN_SPLITS)[:-1])
    qa, ka, va, qb, kb, vb, ga, gb = jnp.split(h, offs, axis=-1)
    slopes = alibi_slopes()
    ya = swa_sink_attention(qa.reshape(B, S, SWA_HEADS, HEAD_DIM),
                            ka.reshape(B, S, SWA_KV_HEADS, HEAD_DIM),
                            va.reshape(B, S, SWA_KV_HEADS, HEAD_DIM),
                            sinks, slopes[:SWA_HEADS])
    yb = moba_attention(qb.reshape(B, S, MOBA_HEADS, HEAD_DIM),
                        kb.reshape(B, S, MOBA_KV_HEADS, HEAD_DIM),
                        vb.reshape(B, S, MOBA_KV_HEADS, HEAD_DIM),
                        slopes[SWA_HEADS:])
    ya = ya.reshape(B, S, SWA_Q) @ w_branch_a
    yb = yb.reshape(B, S, MOBA_Q) @ w_branch_b
    y = jax.nn.sigmoid(ga) * ya + jax.nn.sigmoid(gb) * yb
    return y @ w_o


def setup_inputs(seed: int = 0) -> dict:
    key = jax.random.key(seed)
    ks = jax.random.split(key, 18)
    L, D = DEPTH, D_MODEL

    def nrm(k, shape, fan_in, gain=1.0):
        return jax.random.normal(k, shape, jnp.float32) * (gain * fan_in ** -0.5)

    def gain(k):
        return 1.0 + 0.02 * jax.random.normal(k, (L, D), jnp.float32)

    def bias(k):
        return 0.02 * jax.random.normal(k, (L, D), jnp.float32)

    col_scale = jnp.concatenate([
        jnp.ones((SWA_Q + SWA_KV,), jnp.float32), jnp.full((SWA_KV,), BETA, jnp.float32),
        jnp.ones((MOBA_Q + MOBA_KV,), jnp.float32), jnp.full((MOBA_KV,), BETA, jnp.float32),
        jnp.ones((2 * D,), jnp.float32)])
    return {
        "x": jax.random.normal(ks[0], (BATCH, SEQ, D), jnp.float32),
        "ffn1_w_in": nrm(ks[1], (L, D, 2 * D_FF), D),
        "ffn1_w_out": nrm(ks[2], (L, D_FF, D), D_FF, BETA),
        "ln1_g": gain(ks[3]),
        "ln1_b": bias(ks[4]),
        "mix_w_in": nrm(ks[5], (L, D, IN_COLS), D) * col_scale,
        "swa_sinks": 0.5 * jax.random.normal(ks[6], (L, SWA_HEADS), jnp.float32),
        "w_branch_a": nrm(ks[7], (L, SWA_Q, D), SWA_Q),
        "w_branch_b": nrm(ks[8], (L, MOBA_Q, D), MOBA_Q),
        "mix_w_o": nrm(ks[9], (L, D, D), D, BETA),
        "ln2_g": gain(ks[10]),
        "ln2_b": bias(ks[11]),
        "ffn2_w_in": nrm(ks[12], (L, D, 2 * D_FF), D),
        "ffn2_w_out": nrm(ks[13], (L, D_FF, D), D_FF, BETA),
        "ln3_g": gain(ks[14]),
        "ln3_b": bias(ks[15]),
    }


def reference(x, ffn1_w_in, ffn1_w_out, ln1_g, ln1_b, mix_w_in, swa_sinks, w_branch_a, w_branch_b,
              mix_w_o, ln2_g, ln2_b, ffn2_w_in, ffn2_w_out, ln3_g, ln3_b):
    for l in range(DEPTH):
        x = layer_norm(ALPHA * x + 0.5 * swiglu_ffn(x, ffn1_w_in[l], ffn1_w_out[l]), ln1_g[l], ln1_b[l])
        x = layer_norm(ALPHA * x + hybrid_mixer(x, mix_w_in[l], swa_sinks[l], w_branch_a[l],
                                                w_branch_b[l], mix_w_o[l]), ln2_g[l], ln2_b[l])
        x = layer_norm(ALPHA * x + 0.5 * swiglu_ffn(x, ffn2_w_in[l], ffn2_w_out[l]), ln3_g[l], ln3_b[l])
    return x
```

```python
import numpy as np
from contextlib import ExitStack
import concourse.bass as bass
import concourse.mybir as mybir
from concourse.bass_utils import run_bass_kernel_spmd

F32 = mybir.dt.float32
BF16 = mybir.dt.bfloat16
AF = mybir.ActivationFunctionType
ALU = mybir.AluOpType
AX = mybir.AxisListType

S = 2048
D = 1024
DFF = 2816
NCH = 22
NT = 16
ALPHA = 2.0 ** 0.25
EPS2 = 1e-5 / (ALPHA * ALPHA)
C_FFN = 0.5 / ALPHA
C_MIX = 1.0 / ALPHA
GROUPS = [(0, 3), (3, 3), (6, 3), (9, 13)]
GMAX = 13
W2SLOT = [3, 13]
NEG_SEL = -30000.0
MIXSTOP = 0
LN1_POOL = [False]
STRICT_SAME_ENGINE = True
LNB_ENG = "pool"
SKIP = set()
SCR_WORDS = 26360


class _Op:
    __slots__ = ("eng", "fn", "deps", "signal", "sem", "val", "dma", "idx", "semkey")


class _DSem:
    def __init__(self, sem, key):
        self.sem = sem
        self.key = key
        self.count = 0


class Prog:
    ENGS = ("pe", "act", "dve", "pool", "sp")

    def __init__(self, nc, es):
        self.nc = nc
        self.es = es
        self.streams = {k: [] for k in self.ENGS}
        self.lastw = {}
        self.readers = {}
        self.n = 0
        self.default_reads = []
        self.nsem = 0
        self.bank_rr = 0
        self.bank_pool = list(range(8))

    def dsem(self):
        self.nsem += 1
        s = self.es.enter_context(self.nc.semaphore("dsem%d" % self.nsem))
        return _DSem(s, "d%d" % self.nsem)

    def bank(self):
        pool = self.bank_pool
        b = pool[self.bank_rr % len(pool)]
        self.bank_rr += 1
        return b

    def add(self, eng, fn, reads=(), writes=(), dsem=None):
        o = _Op()
        o.eng = eng
        o.fn = fn
        o.dma = dsem
        o.signal = dsem is not None
        o.sem = None
        o.val = 0
        o.idx = self.n
        self.n += 1
        reads = list(reads) + self.default_reads
        writes = list(writes) + [t for t in reads if isinstance(t, tuple) and t[0] == "ps" and t not in writes]
        deps = {}
        for t in reads:
            w = self.lastw.get(t)
            if w is not None:
                deps[w.idx] = (w, True)
        for t in writes:
            w = self.lastw.get(t)
            if w is not None and w.idx not in deps:
                deps[w.idx] = (w, False)
            rd = self.readers.get(t)
            if rd:
                for r in rd[0].values():
                    if r.idx not in deps:
                        deps[r.idx] = (r, False)
                for r in rd[1]:
                    if r.idx not in deps:
                        deps[r.idx] = (r, False)
        dl = []
        for w, raw in deps.values():
            if w is o:
                continue
            if w.dma is not None and dsem is not None and w.dma is dsem:
                continue
            if w.dma is None and w.eng == eng:
                if eng == "pe":
                    continue
                if dsem is None and not raw and not STRICT_SAME_ENGINE:
                    continue
            dl.append(w)
            w.signal = True
        o.deps = dl
        for t in reads:
            rd = self.readers.get(t)
            if rd is None:
                rd = ({}, [])
                self.readers[t] = rd
            if dsem is None:
                rd[0][eng] = o
            else:
                rd[1].append(o)
        for t in writes:
            self.lastw[t] = o
            self.readers[t] = ({}, [])
        self.streams[eng].append(o)
        return o

    def emit(self):
        nc = self.nc
        esem = {k: self.es.enter_context(nc.semaphore("esem_" + k)) for k in self.ENGS}
        for k, st in self.streams.items():
            c = 0
            for o in st:
                if o.dma is not None:
                    o.dma.count += 16
                    o.sem = o.dma.sem
                    o.val = o.dma.count
                    o.semkey = o.dma.key
                elif o.signal:
                    c += 1
                    o.sem = esem[k]
                    o.val = c
                    o.semkey = k
        streams = self.streams

        def mk(name):
            def body(e):
                waited = {}
                for o in streams[name]:
                    for d in sorted(o.deps, key=lambda d: -d.val):
                        if waited.get(d.semkey, 0) >= d.val:
                            continue
                        e.wait_ge(d.sem, d.val)
                        waited[d.semkey] = d.val
                    ins = o.fn(e)
                    if o.signal:
                        ins.then_inc(o.sem, 16 if o.dma is not None else 1)
            return body

        with nc.Block() as block:
            block.tensor(mk("pe"))
            block.scalar(mk("act"))
            block.vector(mk("dve"))
            block.gpsimd(mk("pool"))
            block.sync(mk("sp"))


def build(stage=3):
    nc = bass.Bass("TRN2", target_bir_lowering=False)
    es = ExitStack()
    with es:
        _build(nc, es, stage)
    return nc


def _build(nc, es, stage):
    def din(name, shape):
        return nc.dram_tensor(name, list(shape), F32, kind="ExternalInput").ap()

    x_d = din("x", [S, D])
    w1_d = [din("w1_%d" % f, [NCH, 128, 8, 256]) for f in range(2)]
    w2_d = [din("w2_%d" % f, [NCH, 128, 1024]) for f in range(2)]
    lngb_d = din("lngb", [6, D])
    wqk_d = din("wqk", [4, 128, 8, 320])
    wv_d = din("wv", [128, 8, 256])
    outw_d = din("outw", [8, 128, 3072])
    wo_d = din("wo", [128, 8, 1024])
    ident_d = din("ident", [128, 128])
    swam_d = din("swam", [128, 8 * 2 * 128])
    tb_d = din("tb", [128, 128])
    tri_d = din("tri", [128, 512])
    kaug_d = din("kaug", [15, S])
    qaug_d = din("qaug", [2, 7, 16 * 512])
    sinks_d = din("sinks", [1, 8])
    y_d = nc.dram_tensor("y", [S, D], F32, kind="ExternalOutput").ap()

    def sb(name, shape, dt):
        return es.enter_context(nc.sbuf_tensor(name, list(shape), dt))

    xres = sb("xres", [128, NT, D], F32)
    xT = sb("xT", [128, 8, S], BF16)
    ident = sb("ident_sb", [128, 128], F32)
    lngb = sb("lngb_sb", [128, 2, D], F32)
    bnst = sb("bnst", [128, 2, 12], F32)
    mv = sb("mv", [128, 2, 2], F32)
    stdt = sb("stdt", [128, 2, 1], F32)
    rstd = sb("rstd", [128, 2, 1], F32)
    nmr = sb("nmr", [128, 2, 1], F32)
    scr = sb("scr", [128, SCR_WORDS], F32)
    ps = [es.enter_context(nc.psum_tensor("ps%d" % i, [128, 512], F32)) for i in range(8)]

    P = Prog(nc, es)

    def carve(off, parts, shape, dt):
        n = int(np.prod(shape))
        words = n if dt == F32 else (n + 1) // 2
        ap = scr[parts[0]:parts[1], off:off + words]
        if dt != F32:
            ap = ap.bitcast(dt)
        if len(shape) == 2:
            ap = ap.rearrange("p (a b) -> p a b", a=shape[0], b=shape[1])
        elif len(shape) == 3:
            ap = ap.rearrange("p (a b c) -> p a b c", a=shape[0], b=shape[1], c=shape[2])
        return ap, off + words

    o = 0
    gT, o = carve(o, (0, 128), [GMAX, S], BF16)
    w2s = []
    for i in range(2):
        a, o = carve(o, (0, 128), [W2SLOT[i], 1024], BF16)
        w2s.append(a)
    w1s = []
    for i in range(3):
        a, o = carve(o, (0, 128), [8, 256], BF16)
        w1s.append(a)
    sls = []
    for i in range(2):
        a, o = carve(o, (0, 128), [512], F32)
        sls.append(a)
    assert o <= SCR_WORDS

    o = 0
    oTA, o = carve(o, (0, 128), [4, S], BF16)
    oTB, o = carve(o, (0, 128), [4, S], BF16)
    o_att0 = o
    qbufs, kbufs = [], []
    for i in range(2):
        a, o = carve(o, (0, 79), [16, 512], BF16)
        qbufs.append(a)
    a, o = carve(o, (0, 79), [S], BF16)
    kbufs.append(a)
    VA, o = carve(o, (0, 128), [16, 2, 66], BF16)
    VB, o = carve(o, (0, 128), [16, 2, 66], BF16)
    o_wq = o
    wqs = []
    for i in range(2):
        a, o = carve(o, (0, 128), [8, 320], BF16)
        wqs.append(a)
    o_wv = o
    wvs, o = carve(o, (0, 128), [8, 256], BF16)
    a, _ = carve(o_wv, (0, 79), [S], BF16)
    kbufs.append(a)
    swam, o = carve(o, (0, 128), [8, 2, 128], BF16)
    tri, o = carve(o, (0, 128), [4, 128], BF16)
    PTs = []
    for i in range(4):
        a, o = carve(o, (0, 128), [512], BF16)
        PTs.append(a)
    ons = []
    for i in range(2):
        a, o = carve(o, (0, 128), [256], F32)
        ons.append(a)
    rdn, o = carve(o, (0, 128), [4], F32)
    dsum, o = carve(o, (0, 128), [4], F32)
    gate_sb, o = carve(o, (0, 128), [4, 8], F32)
    top8, o = carve(o, (0, 128), [4, 8], F32)
    selbpad, o = carve(o, (0, 128), [96], F32)
    ksums, kmeans = [], []
    for i in range(2):
        a, o = carve(o, (0, 128), [8], F32)
        ksums.append(a)
        a, o = carve(o, (0, 128), [8], BF16)
        kmeans.append(a)
    expsink, o = carve(o, (0, 128), [8], F32)
    assert o <= SCR_WORDS, o
    o = o_att0
    yT, o = carve(o, (0, 128), [8, 1024], BF16)
    wos, o = carve(o, (0, 128), [8, 1024], BF16)
    sAs, sBs = [], []
    for i in range(2):
        a, o = carve(o, (0, 128), [512], F32)
        sAs.append(a)
        a, o = carve(o, (0, 128), [512], F32)
        sBs.append(a)
    t1s, o = carve(o, (0, 128), [512], F32)
    t2s, o = carve(o, (0, 128), [512], F32)
    assert o <= o_wq, (o, o_wq)
    o = o_wq
    oss = []
    for i in range(2):
        a, o = carve(o, (0, 128), [3072], BF16)
        oss.append(a)
    assert o <= SCR_WORDS, o

    def dma(q, out, in_, writes, ds, reads=(), **kw):
        return P.add(q, lambda e: e.dma_start(out=out, in_=in_, **kw), reads=reads, writes=writes, dsem=ds)

    def cast_dma(out, in_, writes, ds, reads=()):
        return dma("pool", out, in_, writes, ds, reads=reads, max_dma_last_dim=4096)

    def mm(out, pairs, reads, writes):
        def fn(e):
            n = len(pairs)
            ins = None
            for i, (l, r) in enumerate(pairs):
                ins = e.matmul(out, lhsT=l, rhs=r, start=(i == 0), stop=(i == n - 1))
            return ins
        return P.add("pe", fn, reads=reads, writes=writes)

    def xT_tokens(tt):
        return [("xT", 4 * tt + i, hf) for i in range(4) for hf in range(2)]

    evac_flip = [0]

    def transpose_tile(t, force=None):
        for hf in range(2):
            b = P.bank()

            def fn(e, t=t, hf=hf, b=b):
                ins = None
                for k in range(4):
                    kc = hf * 4 + k
                    ins = e.transpose(ps[b][:, k * 128:(k + 1) * 128], xres[:, t, kc * 128:(kc + 1) * 128], ident[:, :])
                return ins
            P.add("pe", fn, reads=[("xres", t), "ident"], writes=[("ps", b)])
            src = ps[b][:, :].rearrange("p (k t) -> p k t", k=4)
            dst = xT[:, hf * 4:(hf + 1) * 4, t * 128:(t + 1) * 128]
            evac_flip[0] ^= 1
            if force == "act" or (force is None and evac_flip[0]):
                P.add("act", lambda e, s=src, d=dst: e.activation(out=d, in_=s, func=AF.Copy),
                      reads=[("ps", b)], writes=[("xT", t, hf)])
            else:
                P.add("dve", lambda e, s=src, d=dst: e.tensor_copy(out=d, in_=s),
                      reads=[("ps", b)], writes=[("xT", t, hf)])

    ds_ln = [P.dsem(), P.dsem()]

    def load_ln(k):
        dma("sp", lngb[:, 0, :], lngb_d[2 * k:2 * k + 1, :].to_broadcast([128, D]), [("lng",)], ds_ln[0])
        dma("sp", lngb[:, 1, :], lngb_d[2 * k + 1:2 * k + 2, :].to_broadcast([128, D]), [("lnb",)], ds_ln[1])

    ds_out = P.dsem()

    def ln_a(t):
        s = t % 2

        def f1(e):
            e.bn_stats(out=bnst[:, s, 0:6], in_=xres[:, t, 0:512])
            return e.bn_stats(out=bnst[:, s, 6:12], in_=xres[:, t, 512:1024])
        P.add("dve", f1, reads=[("xres", t)], writes=[("bnst", s)])
        P.add("dve", lambda e: e.bn_aggr(out=mv[:, s, :], in_=bnst[:, s, :]), reads=[("bnst", s)], writes=[("mv", s)])
        P.add("act", lambda e: e.activation(out=stdt[:, s, :], in_=mv[:, s, 1:2], func=AF.Sqrt, bias=epsb[:, :], scale=1.0),
              reads=[("mv", s), "epsb"], writes=[("std", s)])

    def ln_b(t):
        s = t % 2
        xt = xres[:, t, :]
        P.add("dve", lambda e: e.reciprocal(out=rstd[:, s, :], in_=stdt[:, s, :]), reads=[("std", s)], writes=[("rstd", s)])
        P.add("dve", lambda e: e.tensor_scalar(out=nmr[:, s, :], in0=mv[:, s, 0:1], scalar1=rstd[:, s, :], scalar2=-1.0,
                                               op0=ALU.mult, op1=ALU.mult),
              reads=[("mv", s), ("rstd", s)], writes=[("nmr", s)])
        P.add("act", lambda e: e.activation(out=xt, in_=xt, func=AF.Identity, bias=nmr[:, s, :], scale=rstd[:, s, :]),
              reads=[("xres", t), ("rstd", s), ("nmr", s)], writes=[("xres", t)])

    def ln_c1(t, final=False):
        xt = xres[:, t, :]
        P.add("dve", lambda e: e.tensor_tensor(out=xt, in0=xt, in1=lngb[:, 0, :], op=ALU.mult),
              reads=[("xres", t), ("lng",)], writes=[("xres", t)])
        P.add("pool" if (final or LN1_POOL[0]) else "dve", lambda e: e.tensor_tensor(out=xt, in0=xt, in1=lngb[:, 1, :], op=ALU.add),
              reads=[("xres", t), ("lnb",)], writes=[("xres", t)])

    def ln_c2(t, final):
        if final:
            dma("sp", y_d[t * 128:(t + 1) * 128, :], xres[:, t, :], [("out", t)], ds_out, reads=[("xres", t)])
        else:
            transpose_tile(t, force="act")

    def ln_step(t, t0, t1, final):
        if t - 3 >= t0:
            ln_c2(t - 3, final)
        ln_a(t)
        if t - 1 >= t0:
            ln_b(t - 1)
        if t - 2 >= t0:
            ln_c1(t - 2, final)
        if t == t1:
            ln_b(t)
            if t - 1 >= t0:
                ln_c1(t - 1, final)
            ln_c1(t, final)
            for k in (t - 2, t - 1, t):
                if k >= t0:
                    ln_c2(k, final)

    epsb = sb("epsb", [128, 1], F32)
    ds_c0 = P.dsem()
    dma("sp", ident[:, :], ident_d[:, :], ["ident"], ds_c0)
    P.add("dve", lambda e: e.memset(epsb[:, :], EPS2), writes=["epsb"])

    ds_x = [P.dsem() for _ in range(NT)]
    for t in range(NT):
        dma("sp", xres[:, t, :], x_d[t * 128:(t + 1) * 128, :], [("xres", t)], ds_x[t])
    xt_done = set()

    def need_xT(tt):
        if tt not in xt_done:
            xt_done.add(tt)
            for t in range(4 * tt, 4 * tt + 4):
                transpose_tile(t)

    need_xT(0)

    def ffn(f, ln_k, final, gate_tok):
        ds_w1 = [P.dsem() for _ in range(3)]
        ds_w2 = [P.dsem() for _ in range(2)]
        load_ln(ln_k)
        w1_issued = [0]

        def issue_w1(i):
            if i < NCH and i == w1_issued[0]:
                s = i % 3
                cast_dma(w1s[s], w1_d[f][i], [("w1", s)], ds_w1[s])
                w1_issued[0] += 1

        issue_w1(0)
        issue_w1(1)
        slc = [0]
        for gi, (c0, G) in enumerate(GROUPS):
            ws = gi % 2
            cast_dma(w2s[ws][:, 0:G, :], w2_d[f][c0:c0 + G].rearrange("g p n -> p g n"), [("w2", ws)], ds_w2[ws])
            for il in range(G):
                i = c0 + il
                issue_w1(i + 2)
                s = i % 3
                for tt in range(4):
                    if f == 0:
                        need_xT(tt)
                    bA = P.bank()
                    bU = P.bank()
                    rhs = [xT[:, kc, tt * 512:(tt + 1) * 512] for kc in range(8)]
                    mm(ps[bA][:, :], [(w1s[s][:, kc, 0:128], rhs[kc]) for kc in range(8)],
                       [("w1", s)] + xT_tokens(tt), [("ps", bA)])
                    mm(ps[bU][:, :], [(w1s[s][:, kc, 128:256], rhs[kc]) for kc in range(8)],
                       [("w1", s)] + xT_tokens(tt), [("ps", bU)])
                    k = slc[0] % 2
                    slc[0] += 1
                    P.add("act", lambda e, k=k, bA=bA: e.activation(out=sls[k], in_=ps[bA][:, :], func=AF.Silu),
                          reads=[("ps", bA)], writes=[("sl", k)])
                    P.add("dve", lambda e, k=k, bU=bU, il=il, tt=tt: e.tensor_tensor(
                        out=gT[:, il, tt * 512:(tt + 1) * 512], in0=sls[k], in1=ps[bU][:, :], op=ALU.mult),
                        reads=[("sl", k), ("ps", bU)], writes=[("gT", il, tt)])
            last = gi == len(GROUPS) - 1
            for t in range(NT):
                for hf in range(2):
                    b = P.bank()
                    mm(ps[b][:, :], [(gT[:, il, t * 128:(t + 1) * 128], w2s[ws][:, il, hf * 512:(hf + 1) * 512]) for il in range(G)],
                       [("gT", il, t // 4) for il in range(G)] + [("w2", ws)], [("ps", b)])
                    wr = [("xres", t)] + ([gate_tok] if last else [])
                    P.add("dve", lambda e, b=b, t=t, hf=hf: e.scalar_tensor_tensor(
                        out=xres[:, t, hf * 512:(hf + 1) * 512], in0=ps[b][:, :], scalar=C_FFN,
                        in1=xres[:, t, hf * 512:(hf + 1) * 512], op0=ALU.mult, op1=ALU.add),
                        reads=[("ps", b), ("xres", t)], writes=wr)
                if last:
                    ln_step(t, 0, NT - 1, final)

    LN1_POOL[0] = True
    ffn(0, 0, stage == 1, "GATE1")
    LN1_POOL[0] = False


    swam_f = swam.rearrange("p a b c -> p (a b c)")
    tri_f = tri.rearrange("p a b -> p (a b)")
    gate_f = gate_sb.rearrange("p a b -> p (a b)")
    rot = {"pt": 0, "e32": 0, "ev": 0, "sg": 0, "acc": 0, "on": 0}

    def evac(dst, src, reads, writes, eng=None):
        if eng is None:
            rot["ev"] ^= 1
            eng = "act" if rot["ev"] else "dve"
        if eng == "act":
            P.add("act", lambda e: e.activation(out=dst, in_=src, func=AF.Copy), reads=reads, writes=writes)
        else:
            P.add("dve", lambda e: e.tensor_copy(out=dst, in_=src), reads=reads, writes=writes)

    def v4(ap):
        return ap.rearrange("p (h q) -> p h q", h=4)

    def mixer(final):
        P.default_reads = ["GATE1"]
        ds_c = P.dsem()
        ds_cp = [P.dsem() for _ in range(4)]
        ds_wq = [P.dsem(), P.dsem()]
        ds_wv = P.dsem()
        ds_qa = [P.dsem(), P.dsem()]
        ds_os = [P.dsem(), P.dsem()]
        load_ln(1)
        cast_dma(wvs, wv_d[:, :, :], [("wv",)], ds_wv)
        cast_dma(wqs[0], wqk_d[0], [("wq", 0)], ds_wq[0])
        cast_dma(kbufs[0][64:79, :], kaug_d[:, :], [("kTaug", 0)], ds_cp[0])
        cast_dma(swam_f, swam_d[:, :], ["swam"], ds_cp[1])
        cast_dma(tri_f, tri_d[:, :], ["tri"], ds_cp[2])
        dma("sp", expsink[:, :], sinks_d[0:1, :].to_broadcast([128, 8]), ["sinkraw"], ds_c)
        P.add("act", lambda e: e.activation(out=expsink[:, :], in_=expsink[:, :], func=AF.Exp),
              reads=["sinkraw"], writes=["expsink"])

        def fc(e):
            e.memset(selbpad, 0.0)
            e.memset(VA[:, :, :, 64:65], 1.0)
            return e.memset(VB[:, :, :, 64:65], 1.0)
        P.add("dve", fc, writes=["selb", "vconst"])

        for t in range(NT):
            b = P.bank()
            mm(ps[b][:, 0:256], [(xT[:, kc, t * 128:(t + 1) * 128], wvs[:, kc, :]) for kc in range(8)],
               [("wv",), ("xT", t, 0), ("xT", t, 1)], [("ps", b)])
            ve = "act" if t % 2 == 0 else "dve"
            ex = ["vdone"] if t == NT - 1 else []
            evac(VA[:, t, :, 0:64], ps[b][:, 0:128].rearrange("p (g d) -> p g d", g=2), [("ps", b)], [("VA", t)] + ex, eng=ve)
            evac(VB[:, t, :, 0:64], ps[b][:, 128:256].rearrange("p (g d) -> p g d", g=2), [("ps", b)], [("VB", t)] + ex, eng=ve)
        cast_dma(kbufs[1][64:79, :], kaug_d[:, :], [("kTaug", 1)], ds_cp[3], reads=["vdone"])

        def proj_thunks(br, g, bi, ev):
            s = bi
            qTf, kT = qbufs[bi], kbufs[bi]
            th = []
            if br == 1:
                def t0():
                    cast_dma(qTf[72:79, :, :], qaug_d[g].rearrange("r (c n) -> r c n", c=16), [("qalibi", bi)], ds_qa[bi])
                    P.add("dve", lambda e: e.memset(qTf[64:72, 0:8, :], 0.0), writes=[("qaug0", bi)])
                th.append(t0)
            for u in (4, 0, 1, 2, 3):
                for tt in range(4):
                    def tk(u=u, tt=tt):
                        b = P.bank()
                        mm(ps[b][0:64, :], [(wqs[s][:, kc, u * 64:(u + 1) * 64], xT[:, kc, tt * 512:(tt + 1) * 512]) for kc in range(8)],
                           [("wq", s)] + xT_tokens(tt), [("ps", b)])
                        if u < 4:
                            evac(qTf[0:64, 4 * tt:4 * tt + 4, u * 128:(u + 1) * 128],
                                 ps[b][0:64, :].rearrange("p (c q) -> p c q", c=4), [("ps", b)], [("qT", bi, tt, u)], eng=ev)
                        else:
                            kex = ["vdone"] if bi == 1 else []
                            evac(kT[0:64, tt * 512:(tt + 1) * 512], ps[b][0:64, :], [("ps", b)] + kex, [("kT", bi, tt)], eng="dve")
                            if br == 1:
                                P.add("dve", lambda e: e.tensor_reduce(
                                    out=ksums[g][0:64, 2 * tt:2 * tt + 2], in_=ps[b][0:64, :].rearrange("p (a b) -> p a b", a=2),
                                    axis=AX.X, op=ALU.add), reads=[("ps", b)], writes=[("ksum", g, tt)])
                                if tt == 3:
                                    P.add("dve", lambda e: e.tensor_scalar(out=kmeans[g][0:64, :], in0=ksums[g][0:64, :],
                                                                           scalar1=1.0 / 256.0, scalar2=None, op0=ALU.mult),
                                          reads=[("ksum", g, t2) for t2 in range(4)], writes=[("kmean", g)])
                    th.append(tk)
            return th

        def drain(th, n):
            for _ in range(min(n, len(th))):
                th.pop(0)()

        def finish_chunk(k, oT, g, c, tokname, eng):
            bt = P.bank()

            def ft(e):
                ins = None
                for k2 in range(2):
                    ins = e.transpose(ps[bt][:, k2 * 128:(k2 + 1) * 128], ons[k][:, k2 * 128:(k2 + 1) * 128], ident[:, :])
                return ins
            P.add("pe", ft, reads=[("on", k), "ident"], writes=[("ps", bt)])
            evac(oT[:, 2 * g:2 * g + 2, c * 128:(c + 1) * 128], ps[bt][:, 0:256].rearrange("p (a q) -> p a q", a=2),
                 [("ps", bt)], [(tokname, g, c), "GATE2"], eng=eng)

        def swa(g, bi, nxt):
            qTf, kT = qbufs[bi], kbufs[bi]
            stA = {}

            def qtoks(c):
                return [("qT", bi, c // 4, u) for u in range(4)]

            def stageA(nb):
                js = [nb] if nb == 0 else [nb - 1, nb]
                pts = []
                for j in js:
                    w = 0 if j == nb else 1
                    bs = P.bank()
                    mm(ps[bs][:, :], [(kT[0:64, j * 128:(j + 1) * 128], qTf[0:64, nb, :])],
                       [("kT", bi, j // 4)] + qtoks(nb), [("ps", bs)])
                    pi = rot["pt"] % 4
                    rot["pt"] += 1
                    P.add("act", lambda e, bs=bs, pi=pi: e.activation(out=PTs[pi], in_=ps[bs][:, :], func=AF.Exp, scale=0.125),
                          reads=[("ps", bs)], writes=[("PT", pi)])
                    P.add("dve", lambda e, pi=pi, w=w: e.tensor_tensor(
                        out=v4(PTs[pi]), in0=v4(PTs[pi]), in1=swam[:, 4 * g:4 * g + 4, w, :], op=ALU.mult),
                        reads=[("PT", pi), "swam"], writes=[("PT", pi)])
                    pts.append((j, pi))
                stA[nb] = pts

            def stageB(nb):
                pts = stA.pop(nb)
                bo = 4 + rot["acc"] % 2
                rot["acc"] += 1

                def fpv(e):
                    ins = None
                    for hl in range(4):
                        for k, (j, pi) in enumerate(pts):
                            ins = e.matmul(ps[bo][:, hl * 65:(hl + 1) * 65], lhsT=PTs[pi][:, hl * 128:(hl + 1) * 128],
                                           rhs=VA[:, j, g, 0:65], start=(k == 0), stop=(k == len(pts) - 1))
                    return ins
                P.add("pe", fpv, reads=[("PT", pi) for _, pi in pts] + [("VA", j) for j, _ in pts] + ["vconst"],
                      writes=[("ps", bo)])
                k = rot["on"] % 2
                rot["on"] += 1
                P.add("dve", lambda e: e.tensor_tensor(
                    out=dsum.rearrange("p (h o) -> p h o", o=1),
                    in0=ps[bo][:, 0:260].rearrange("p (h d) -> p h d", h=4)[:, :, 64:65],
                    in1=expsink[:, 4 * g:4 * g + 4].rearrange("p (h o) -> p h o", o=1), op=ALU.add),
                    reads=[("ps", bo), "expsink"], writes=["dsum"])
                P.add("dve", lambda e: e.reciprocal(out=rdn, in_=dsum), reads=["dsum"], writes=["rd"])

                def fn(e):
                    ins = None
                    for hl in range(4):
                        ins = e.tensor_scalar(out=ons[k][:, hl * 64:(hl + 1) * 64], in0=ps[bo][:, hl * 65:hl * 65 + 64],
                                              scalar1=rdn[:, hl:hl + 1], scalar2=None, op0=ALU.mult)
                    return ins
                P.add("dve", fn, reads=[("ps", bo), "rd"], writes=[("on", k)])
                return (k, oTA, g, nb, "oTA", "act")

            pendC = []
            per = -(-len(nxt) // NT)
            for step in range(NT + 2):
                if step < NT:
                    stageA(step)
                if 0 <= step - 1 < NT:
                    pendC.append(stageB(step - 1))
                if 0 <= step - 2 < NT:
                    finish_chunk(*pendC.pop(0))
                drain(nxt, per)
            drain(nxt, len(nxt))

        def moba(g, bi, nxt):
            qTf, kT = qbufs[bi], kbufs[bi]
            kmean = kmeans[g]

            def qtoks(c):
                return [("qT", bi, c // 4, u) for u in range(4)]
            P.add("dve", lambda e: e.memset(gate_f, -1.0e30), writes=["gate_sb"])

            def part1(c):
                qb = c // 2
                bg = P.bank()

                def fg(e):
                    ins = None
                    for hl in range(4):
                        ins = e.matmul(ps[bg][:, hl * 8:hl * 8 + qb], lhsT=qTf[0:64, c, hl * 128:(hl + 1) * 128],
                                       rhs=kmean[0:64, 0:qb], start=True, stop=True)
                    return ins
                P.add("pe", fg, reads=qtoks(c) + [("kmean", g)], writes=[("ps", bg)])
                P.add("dve", lambda e: e.tensor_copy(out=gate_sb[:, :, 0:qb],
                                                     in_=ps[bg][:, 0:32].rearrange("p (h n) -> p h n", h=4)[:, :, 0:qb]),
                      reads=[("ps", bg), "gate_sb"], writes=["gate_sb"])

                def fm(e):
                    ins = None
                    for hl in range(4):
                        ins = e.max(out=top8[:, hl, :], in_=gate_sb[:, hl, :])
                    return ins
                P.add("dve", fm, reads=["gate_sb"], writes=["top8"])

                def fs(e):
                    ins = None
                    for hl in range(4):
                        ins = e.tensor_scalar(out=selbpad[:, 64 + 8 * hl:72 + 8 * hl], in0=gate_sb[:, hl, :],
                                              scalar1=top8[:, hl, 2:3], scalar2=NEG_SEL, op0=ALU.is_lt, op1=ALU.mult)
                    return ins
                P.add("dve", fs, reads=["gate_sb", "top8"], writes=["selb"])
                P.add("dve", lambda e: e.memset(selbpad[:, 64:96].rearrange("p (h n) -> p h n", h=4)[:, :, qb:qb + 1], 0.0),
                      writes=["selb"])

            def part2(c):
                bt = P.bank()

                def ft(e):
                    ins = None
                    for hl in range(4):
                        ins = e.transpose(ps[bt][0:72, hl * 128:(hl + 1) * 128], selbpad[:, 8 * hl:8 * hl + 72], ident[:, :])
                    return ins
                P.add("pe", ft, reads=["selb", "ident"], writes=[("ps", bt)])
                P.add("dve", lambda e: e.tensor_copy(out=qTf[64:72, c, :], in_=ps[bt][64:72, :]),
                      reads=[("ps", bt)], writes=[("qaug", bi, c)])

            prev = []

            def attn(c):
                js = list(range(c + 1))
                qa = ("qaug", bi, c) if c >= 8 else ("qaug0", bi)
                pend = []

                def emit_pv():
                    idx, j, pi = pend.pop()

                    def fpv(e):
                        ins = None
                        for hl in range(4):
                            ins = e.matmul(ps[4 + hl][:, 0:65], lhsT=PTs[pi][:, hl * 128:(hl + 1) * 128], rhs=VB[:, j, g, 0:65],
                                           start=(idx == 0), stop=(idx == len(js) - 1))
                        return ins
                    P.add("pe", fpv, reads=[("PT", pi), ("VB", j), "vconst"], writes=[("ps", 4 + hl) for hl in range(4)])
                LOOK = 2
                n = len(js)
                for idx in range(n + LOOK):
                    if idx < n:
                        j = js[idx]
                        bs = P.bank()
                        mm(ps[bs][:, :], [(kT[0:79, j * 128:(j + 1) * 128], qTf[0:79, c, :])],
                           [("kT", bi, j // 4), ("kTaug", bi), ("qalibi", bi), qa] + qtoks(c), [("ps", bs)])
                        pi = rot["pt"] % 4
                        rot["pt"] += 1
                        P.add("act", lambda e, bs=bs, pi=pi: e.activation(out=PTs[pi], in_=ps[bs][:, :], func=AF.Exp, scale=0.125),
                              reads=[("ps", bs)], writes=[("PT", pi)])
                        if j == c:
                            P.add("dve", lambda e, pi=pi: e.tensor_tensor(out=v4(PTs[pi]), in0=v4(PTs[pi]), in1=tri, op=ALU.mult),
                                  reads=[("PT", pi), "tri"], writes=[("PT", pi)])
                        pend.insert(0, (idx, j, pi))
                    if idx == 1 and prev:
                        finish_chunk(*prev.pop())
                    if idx - LOOK >= 0:
                        emit_pv()
                if prev:
                    finish_chunk(*prev.pop())
                k = rot["on"] % 2
                rot["on"] += 1

                def fr(e):
                    ins = None
                    for hl in range(4):
                        ins = e.reciprocal(out=rdn[:, hl:hl + 1], in_=ps[4 + hl][:, 64:65])
                    return ins
                P.add("dve", fr, reads=[("ps", 4 + hl) for hl in range(4)], writes=["rd"])

                def fn(e):
                    ins = None
                    for hl in range(4):
                        ins = e.tensor_scalar(out=ons[k][:, hl * 64:(hl + 1) * 64], in0=ps[4 + hl][:, 0:64],
                                              scalar1=rdn[:, hl:hl + 1], scalar2=None, op0=ALU.mult)
                    return ins
                P.add("dve", fn, reads=[("ps", 4 + hl) for hl in range(4)] + ["rd"], writes=[("on", k)])
                prev.append((k, oTB, g, c, "oTB", "dve"))

            tot = sum(c + 1 for c in range(NT))
            done = 0
            n0 = len(nxt)
            for c in range(NT):
                if 8 <= c + 1 < NT:
                    part1(c + 1)
                attn(c)
                if 8 <= c + 1 < NT:
                    part2(c + 1)
                done += c + 1
                want = -(-n0 * done // tot) if c >= 2 else 0
                drain(nxt, max(0, want - (n0 - len(nxt))))
            finish_chunk(*prev.pop())
            drain(nxt, len(nxt))

        P.bank_pool = list(range(4))
        order = [(0, 0), (0, 1), (1, 0), (1, 1)]
        th = proj_thunks(0, 0, 0, None)
        drain(th, len(th))
        for k, (br, g) in enumerate(order):
            bi = k % 2
            nxt = []
            if k + 1 < 4:
                nb_, ng_ = order[k + 1]
                s = (k + 1) % 2
                cast_dma(wqs[s], wqk_d[nb_ * 2 + ng_], [("wq", s)], ds_wq[s])
                nxt = proj_thunks(nb_, ng_, s, "act" if br == 0 else "dve")
            if k == 3:
                cast_dma(oss[0], outw_d[0], [("os", 0), ("wq", 0), ("wq", 1)], ds_os[0])
            if br == 0:
                swa(g, bi, nxt)
            else:
                moba(g, bi, nxt)
        P.bank_pool = list(range(8))

        P.default_reads = ["GATE2"]
        ds_wo = P.dsem()
        cnt = 0
        for th in range(2):
            for dc in range(8):
                s = cnt % 2
                cnt += 1
                if cnt > 1:
                    cast_dma(oss[s], outw_d[dc], [("os", s)], ds_os[s])
                if cnt == 2:
                    cast_dma(wos, wo_d[:, :, :], ["wo"], ds_wo)
                for tl in range(2):
                    tt = 2 * th + tl
                    bga, bgb, bya, byb = P.bank(), P.bank(), P.bank(), P.bank()
                    tok = slice(tt * 512, (tt + 1) * 512)
                    mm(ps[bga][:, :], [(oss[s][:, kc * 128:(kc + 1) * 128], xT[:, kc, tok]) for kc in range(8)],
                       [("os", s)] + xT_tokens(tt), [("ps", bga)])
                    mm(ps[bgb][:, :], [(oss[s][:, 1024 + kc * 128:1024 + (kc + 1) * 128], xT[:, kc, tok]) for kc in range(8)],
                       [("os", s)] + xT_tokens(tt), [("ps", bgb)])
                    mm(ps[bya][:, :], [(oss[s][:, 2048 + hc * 128:2048 + (hc + 1) * 128], oTA[:, hc, tok]) for hc in range(4)],
                       [("os", s)] + [("oTA", g, nb) for g in range(2) for nb in range(4 * tt, 4 * tt + 4)], [("ps", bya)])
                    mm(ps[byb][:, :], [(oss[s][:, 2560 + hc * 128:2560 + (hc + 1) * 128], oTB[:, hc, tok]) for hc in range(4)],
                       [("os", s)] + [("oTB", g, nb) for g in range(2) for nb in range(4 * tt, 4 * tt + 4)], [("ps", byb)])
                    k = rot["sg"] % 2
                    rot["sg"] += 1
                    P.add("act", lambda e, k=k, b=bga: e.activation(out=sAs[k], in_=ps[b][:, :], func=AF.Sigmoid),
                          reads=[("ps", bga)], writes=[("sA", k)])
                    P.add("act", lambda e, k=k, b=bgb: e.activation(out=sBs[k], in_=ps[b][:, :], func=AF.Sigmoid),
                          reads=[("ps", bgb)], writes=[("sB", k)])
                    P.add("dve", lambda e, k=k, b=bya: e.tensor_tensor(out=t1s, in0=sAs[k], in1=ps[b][:, :], op=ALU.mult),
                          reads=[("sA", k), ("ps", bya)], writes=["t1"])
                    P.add("dve", lambda e, k=k, b=byb: e.tensor_tensor(out=t2s, in0=sBs[k], in1=ps[b][:, :], op=ALU.mult),
                          reads=[("sB", k), ("ps", byb)], writes=["t2"])
                    P.add("dve", lambda e, dc=dc, tl=tl: e.tensor_tensor(out=yT[:, dc, tl * 512:(tl + 1) * 512], in0=t1s, in1=t2s,
                                                                        op=ALU.add),
                          reads=["t1", "t2"], writes=[("yT", dc, tl)])
            for t4 in range(8):
                t = 8 * th + t4
                for hf in range(2):
                    b = P.bank()
                    mm(ps[b][:, :], [(yT[:, dc, t4 * 128:(t4 + 1) * 128], wos[:, dc, hf * 512:(hf + 1) * 512]) for dc in range(8)],
                       [("yT", dc, t4 // 4) for dc in range(8)] + ["wo"], [("ps", b)])
                    wr = [("xres", t)] + (["GATE3"] if th == 1 else [])
                    P.add("dve", lambda e, b=b, t=t, hf=hf: e.scalar_tensor_tensor(
                        out=xres[:, t, hf * 512:(hf + 1) * 512], in0=ps[b][:, :], scalar=C_MIX,
                        in1=xres[:, t, hf * 512:(hf + 1) * 512], op0=ALU.mult, op1=ALU.add),
                        reads=[("ps", b), ("xres", t)], writes=wr)
                ln_step(t, 8 * th, 8 * th + 7, final)

    if stage >= 2:
        mixer(stage == 2)
    if stage >= 3:
        P.default_reads = ["GATE3"]
        ffn(1, 2, True, "GATE4")

    P.add("sp", lambda e: e.nop(), reads=[("out", t) for t in range(NT)])
    P.emit()


def _prep_ffn(w_in, w_out):
    a = w_in[:, :DFF].reshape(8, 128, NCH, 128)
    u = w_in[:, DFF:].reshape(8, 128, NCH, 128)
    w1 = np.concatenate([a, u], axis=3).transpose(2, 1, 0, 3)
    return np.ascontiguousarray(w1, dtype=np.float32), np.ascontiguousarray(w_out.reshape(NCH, 128, D), dtype=np.float32)


def _consts():
    i = np.arange(1, 17, dtype=np.float64)
    slopes = np.exp2(-8.0 * i / 16.0)
    sa, sbb = slopes[:8], slopes[8:]
    kj = np.arange(128)[:, None]
    qi = np.arange(128)[None, :]
    swam = np.zeros((128, 8, 2, 128), np.float64)
    d0 = (qi - kj).astype(np.float64)
    d1 = (qi + 128 - kj).astype(np.float64)
    for h in range(8):
        swam[:, h, 0, :] = np.where(d0 >= 0, np.exp(-sa[h] * d0), 0.0)
        swam[:, h, 1, :] = np.where(d1 < 128, np.exp(-sa[h] * d1), 0.0)
    p = np.arange(128, dtype=np.float64)[:, None, None]
    dl = np.arange(16, dtype=np.float64)[None, None, :]
    tb = -sbb[None, :, None] * (dl * 128.0 - p)
    tri = np.broadcast_to((kj <= qi)[:, None, :], (128, 4, 128)).astype(np.float32)
    onehot = (np.arange(S)[None, :] // 256 == np.arange(8)[:, None]).astype(np.float32)
    def bf(a):
        u = np.ascontiguousarray(np.asarray(a, np.float32)).view(np.uint32).astype(np.uint64)
        u = (u + 0x7FFF + ((u >> 16) & 1)) & 0xFFFF0000
        return u.astype(np.uint32).view(np.float32)
    keys = np.arange(S)
    kaug = np.zeros((15, S), np.float32)
    kaug[0:8] = onehot
    kaug[8:11] = keys // 128
    kaug[11:14] = keys % 128
    kaug[14] = 1.0
    sh1 = bf(sbb)
    sh2 = bf(sbb - sh1.astype(np.float64))
    sh3 = bf(sbb - sh1.astype(np.float64) - sh2.astype(np.float64))
    qaug = np.zeros((2, 7, 16, 4, 128), np.float32)
    for g in range(2):
        for hl in range(4):
            h = 4 * g + hl
            qaug[g, 0, :, hl, :] = 1024.0 * sh1[h]
            qaug[g, 1, :, hl, :] = 1024.0 * sh2[h]
            qaug[g, 2, :, hl, :] = 1024.0 * sh3[h]
            qaug[g, 3, :, hl, :] = 8.0 * sh1[h]
            qaug[g, 4, :, hl, :] = 8.0 * sh2[h]
            qaug[g, 5, :, hl, :] = 8.0 * sh3[h]
            qaug[g, 6, :, hl, :] = (-1024.0 * np.arange(16) * sbb[h])[:, None]
    return dict(
        ident=np.eye(128, dtype=np.float32),
        swam=np.ascontiguousarray(swam.reshape(128, -1), dtype=np.float32),
        tb=np.ascontiguousarray(tb.reshape(128, 128), dtype=np.float32),
        tri=np.ascontiguousarray(tri.reshape(128, 512)),
        kaug=np.ascontiguousarray(kaug),
        qaug=np.ascontiguousarray(qaug.reshape(2, 7, 16 * 512)),
    )


def _prep_mix(w_in, wa, wb, wo):
    W = w_in.reshape(8, 128, 3584)

    def cols(lo, n):
        return W[:, :, lo:lo + n]
    qA, kA, vA, qB, kB, vB, gA, gB = 0, 512, 640, 768, 1280, 1408, 1536, 2560
    wqk = np.empty((4, 128, 8, 320), np.float32)
    for br, (q0, k0) in enumerate([(qA, kA), (qB, kB)]):
        for g in range(2):
            blk = np.concatenate([cols(q0 + g * 256, 256), cols(k0 + g * 64, 64)], axis=2)
            wqk[br * 2 + g] = blk.transpose(1, 0, 2)
    wv = np.ascontiguousarray(np.concatenate([cols(vA, 128), cols(vB, 128)], axis=2).transpose(1, 0, 2))
    outw = np.empty((8, 128, 3072), np.float32)
    for dc in range(8):
        ga = cols(gA + dc * 128, 128).transpose(1, 0, 2).reshape(128, 1024)
        gb = cols(gB + dc * 128, 128).transpose(1, 0, 2).reshape(128, 1024)
        a = wa[:, dc * 128:(dc + 1) * 128].reshape(4, 128, 128).transpose(1, 0, 2).reshape(128, 512)
        b = wb[:, dc * 128:(dc + 1) * 128].reshape(4, 128, 128).transpose(1, 0, 2).reshape(128, 512)
        wab = np.concatenate([a, b], axis=1)
        outw[dc] = np.concatenate([ga, gb, wab], axis=1)
    wo_l = np.ascontiguousarray(wo.reshape(8, 128, 1024).transpose(1, 0, 2))
    return wqk, wv, outw, wo_l


def make_in_maps(inputs, n_cores=8):
    f32 = lambda a: np.asarray(a, dtype=np.float32)
    w1_0, w2_0 = _prep_ffn(f32(inputs["ffn1_w_in"])[0], f32(inputs["ffn1_w_out"])[0])
    w1_1, w2_1 = _prep_ffn(f32(inputs["ffn2_w_in"])[0], f32(inputs["ffn2_w_out"])[0])
    wqk, wv, outw, wo_l = _prep_mix(f32(inputs["mix_w_in"])[0], f32(inputs["w_branch_a"])[0],
                                    f32(inputs["w_branch_b"])[0], f32(inputs["mix_w_o"])[0])
    lngb = np.ascontiguousarray(np.stack([f32(inputs[k])[0] for k in
                                          ("ln1_g", "ln1_b", "ln2_g", "ln2_b", "ln3_g", "ln3_b")]))
    shared = dict(w1_0=w1_0, w2_0=w2_0, w1_1=w1_1, w2_1=w2_1, lngb=lngb, wqk=wqk, wv=wv, outw=outw, wo=wo_l,
                  sinks=np.ascontiguousarray(f32(inputs["swa_sinks"]).reshape(1, 8)))
    shared.update(_consts())
    x = f32(inputs["x"])
    maps = []
    for c in range(n_cores):
        m = dict(shared)
        m["x"] = np.ascontiguousarray(x[c])
        maps.append(m)
    return maps


def kernel(**inputs):
    nc = build(3)
    maps = make_in_maps(inputs, 8)
    res = run_bass_kernel_spmd(nc, maps, core_ids=list(range(8)))
    out = np.stack([np.asarray(r["y"], dtype=np.float32) for r in res.results], axis=0)
    return out
```

```python
import numpy as np
from contextlib import ExitStack
import concourse.bass as bass
import concourse.mybir as mybir
from concourse.bass_utils import run_bass_kernel_spmd

F32 = mybir.dt.float32
BF16 = mybir.dt.bfloat16
AF = mybir.ActivationFunctionType
ALU = mybir.AluOpType
AX = mybir.AxisListType

S = 2048
D = 1024
DFF = 2816
NCH = 22
NT = 16
ALPHA = 2.0 ** 0.25
EPS2 = 1e-5 / (ALPHA * ALPHA)
C_FFN = 0.5 / ALPHA
C_MIX = 1.0 / ALPHA
GROUPS = [(0, 3), (3, 3), (6, 3), (9, 13)]
GMAX = 13
W2SLOT = [3, 13]
NEG_SEL = -30000.0
MIXSTOP = 0
UNIT_ORDER = [(1, 0), (1, 1), (0, 0), (0, 1)]
STRICT_SAME_ENGINE = True
LNB_ENG = "pool"
SKIP = set()
SCR_WORDS = 26360


class _Op:
    __slots__ = ("eng", "fn", "deps", "signal", "sem", "val", "dma", "idx", "semkey")


class _DSem:
    def __init__(self, sem, key):
        self.sem = sem
        self.key = key
        self.count = 0


class Prog:
    ENGS = ("pe", "act", "dve", "pool", "sp")

    def __init__(self, nc, es):
        self.nc = nc
        self.es = es
        self.streams = {k: [] for k in self.ENGS}
        self.lastw = {}
        self.readers = {}
        self.n = 0
        self.default_reads = []
        self.nsem = 0
        self.bank_rr = 0
        self.bank_pool = list(range(8))

    def dsem(self):
        self.nsem += 1
        s = self.es.enter_context(self.nc.semaphore("dsem%d" % self.nsem))
        return _DSem(s, "d%d" % self.nsem)

    def bank(self):
        pool = self.bank_pool
        b = pool[self.bank_rr % len(pool)]
        self.bank_rr += 1
        return b

    def add(self, eng, fn, reads=(), writes=(), dsem=None):
        o = _Op()
        o.eng = eng
        o.fn = fn
        o.dma = dsem
        o.signal = dsem is not None
        o.sem = None
        o.val = 0
        o.idx = self.n
        self.n += 1
        reads = list(reads) + self.default_reads
        writes = list(writes) + [t for t in reads if isinstance(t, tuple) and t[0] == "ps" and t not in writes]
        deps = {}
        for t in reads:
            w = self.lastw.get(t)
            if w is not None:
                deps[w.idx] = (w, True)
        for t in writes:
            w = self.lastw.get(t)
            if w is not None and w.idx not in deps:
                deps[w.idx] = (w, False)
            rd = self.readers.get(t)
            if rd:
                for r in rd[0].values():
                    if r.idx not in deps:
                        deps[r.idx] = (r, False)
                for r in rd[1]:
                    if r.idx not in deps:
                        deps[r.idx] = (r, False)
        dl = []
        for w, raw in deps.values():
            if w is o:
                continue
            if w.dma is not None and dsem is not None and w.dma is dsem:
                continue
            if w.dma is None and w.eng == eng:
                if eng == "pe":
                    continue
                if dsem is None and not raw and not STRICT_SAME_ENGINE:
                    continue
            dl.append(w)
            w.signal = True
        o.deps = dl
        for t in reads:
            rd = self.readers.get(t)
            if rd is None:
                rd = ({}, [])
                self.readers[t] = rd
            if dsem is None:
                rd[0][eng] = o
            else:
                rd[1].append(o)
        for t in writes:
            self.lastw[t] = o
            self.readers[t] = ({}, [])
        self.streams[eng].append(o)
        return o

    def emit(self):
        nc = self.nc
        esem = {k: self.es.enter_context(nc.semaphore("esem_" + k)) for k in self.ENGS}
        for k, st in self.streams.items():
            c = 0
            for o in st:
                if o.dma is not None:
                    o.dma.count += 16
                    o.sem = o.dma.sem
                    o.val = o.dma.count
                    o.semkey = o.dma.key
                elif o.signal:
                    c += 1
                    o.sem = esem[k]
                    o.val = c
                    o.semkey = k
        streams = self.streams

        def mk(name):
            def body(e):
                waited = {}
                for o in streams[name]:
                    for d in sorted(o.deps, key=lambda d: -d.val):
                        if waited.get(d.semkey, 0) >= d.val:
                            continue
                        e.wait_ge(d.sem, d.val)
                        waited[d.semkey] = d.val
                    ins = o.fn(e)
                    if o.signal:
                        ins.then_inc(o.sem, 16 if o.dma is not None else 1)
            return body

        with nc.Block() as block:
            block.tensor(mk("pe"))
            block.scalar(mk("act"))
            block.vector(mk("dve"))
            block.gpsimd(mk("pool"))
            block.sync(mk("sp"))


def build(stage=3):
    nc = bass.Bass("TRN2", target_bir_lowering=False)
    es = ExitStack()
    with es:
        _build(nc, es, stage)
    return nc


def _build(nc, es, stage):
    def din(name, shape):
        return nc.dram_tensor(name, list(shape), F32, kind="ExternalInput").ap()

    x_d = din("x", [S, D])
    w1_d = [din("w1_%d" % f, [NCH, 128, 8, 256]) for f in range(2)]
    w2_d = [din("w2_%d" % f, [NCH, 128, 1024]) for f in range(2)]
    lngb_d = din("lngb", [6, D])
    wqk_d = din("wqk", [4, 128, 8, 320])
    wv_d = din("wv", [128, 8, 256])
    outw_d = din("outw", [8, 128, 3072])
    wo_d = din("wo", [128, 8, 1024])
    ident_d = din("ident", [128, 128])
    swam_d = din("swam", [128, 8 * 2 * 128])
    tb_d = din("tb", [128, 128])
    tri_d = din("tri", [128, 512])
    kaug_d = din("kaug", [15, S])
    qaug_d = din("qaug", [2, 7, 16 * 512])
    sinks_d = din("sinks", [1, 8])
    y_d = nc.dram_tensor("y", [S, D], F32, kind="ExternalOutput").ap()

    def sb(name, shape, dt):
        return es.enter_context(nc.sbuf_tensor(name, list(shape), dt))

    xres = sb("xres", [128, NT, D], F32)
    xT = sb("xT", [128, 8, S], BF16)
    ident = sb("ident_sb", [128, 128], F32)
    lngb = sb("lngb_sb", [128, 2, D], F32)
    bnst = sb("bnst", [128, 2, 12], F32)
    mv = sb("mv", [128, 2, 2], F32)
    stdt = sb("stdt", [128, 2, 1], F32)
    rstd = sb("rstd", [128, 2, 1], F32)
    nmr = sb("nmr", [128, 2, 1], F32)
    scr = sb("scr", [128, SCR_WORDS], F32)
    ps = [es.enter_context(nc.psum_tensor("ps%d" % i, [128, 512], F32)) for i in range(8)]

    P = Prog(nc, es)

    def carve(off, parts, shape, dt):
        n = int(np.prod(shape))
        words = n if dt == F32 else (n + 1) // 2
        ap = scr[parts[0]:parts[1], off:off + words]
        if dt != F32:
            ap = ap.bitcast(dt)
        if len(shape) == 2:
            ap = ap.rearrange("p (a b) -> p a b", a=shape[0], b=shape[1])
        elif len(shape) == 3:
            ap = ap.rearrange("p (a b c) -> p a b c", a=shape[0], b=shape[1], c=shape[2])
        return ap, off + words

    o = 0
    gT, o = carve(o, (0, 128), [GMAX, S], BF16)
    w2s = []
    for i in range(2):
        a, o = carve(o, (0, 128), [W2SLOT[i], 1024], BF16)
        w2s.append(a)
    w1s = []
    for i in range(3):
        a, o = carve(o, (0, 128), [8, 256], BF16)
        w1s.append(a)
    sls = []
    for i in range(2):
        a, o = carve(o, (0, 128), [512], F32)
        sls.append(a)
    assert o <= SCR_WORDS

    o = 0
    oTA, o = carve(o, (0, 128), [4, S], BF16)
    oTB, o = carve(o, (0, 128), [4, S], BF16)
    o_att0 = o
    qbufs, kbufs = [], []
    for i in range(2):
        a, o = carve(o, (0, 79), [16, 512], BF16)
        qbufs.append(a)
    a, o = carve(o, (0, 79), [S], BF16)
    kbufs.append(a)
    VA, o = carve(o, (0, 128), [16, 2, 66], BF16)
    VB, o = carve(o, (0, 128), [16, 2, 66], BF16)
    o_wq = o
    wqs = []
    for i in range(2):
        a, o = carve(o, (0, 128), [8, 320], BF16)
        wqs.append(a)
    o_wv = o
    wvs, o = carve(o, (0, 128), [8, 256], BF16)
    a, _ = carve(o_wv, (0, 79), [S], BF16)
    kbufs.append(a)
    swam, o = carve(o, (0, 128), [8, 2, 128], BF16)
    tri, o = carve(o, (0, 128), [4, 128], BF16)
    PTs = []
    for i in range(4):
        a, o = carve(o, (0, 128), [512], BF16)
        PTs.append(a)
    ons = []
    for i in range(2):
        a, o = carve(o, (0, 128), [256], F32)
        ons.append(a)
    rdn, o = carve(o, (0, 128), [4], F32)
    dsum, o = carve(o, (0, 128), [4], F32)
    gate_sb, o = carve(o, (0, 128), [4, 8], F32)
    top8, o = carve(o, (0, 128), [4, 8], F32)
    selbpad, o = carve(o, (0, 128), [96], F32)
    ksums, kmeans = [], []
    for i in range(2):
        a, o = carve(o, (0, 128), [8], F32)
        ksums.append(a)
        a, o = carve(o, (0, 128), [8], BF16)
        kmeans.append(a)
    expsink, o = carve(o, (0, 128), [8], F32)
    assert o <= SCR_WORDS, o
    o = o_att0
    yT, o = carve(o, (0, 128), [8, 1024], BF16)
    wos, o = carve(o, (0, 128), [8, 1024], BF16)
    sAs, sBs = [], []
    for i in range(2):
        a, o = carve(o, (0, 128), [512], F32)
        sAs.append(a)
        a, o = carve(o, (0, 128), [512], F32)
        sBs.append(a)
    t1s, o = carve(o, (0, 128), [512], F32)
    t2s, o = carve(o, (0, 128), [512], F32)
    assert o <= o_wq, (o, o_wq)
    o = o_wq
    oss = []
    for i in range(2):
        a, o = carve(o, (0, 128), [3072], BF16)
        oss.append(a)
    assert o <= SCR_WORDS, o

    def dma(q, out, in_, writes, ds, reads=(), **kw):
        return P.add(q, lambda e: e.dma_start(out=out, in_=in_, **kw), reads=reads, writes=writes, dsem=ds)

    def cast_dma(out, in_, writes, ds, reads=()):
        return dma("pool", out, in_, writes, ds, reads=reads, max_dma_last_dim=4096)

    def mm(out, pairs, reads, writes):
        def fn(e):
            n = len(pairs)
            ins = None
            for i, (l, r) in enumerate(pairs):
                ins = e.matmul(out, lhsT=l, rhs=r, start=(i == 0), stop=(i == n - 1))
            return ins
        return P.add("pe", fn, reads=reads, writes=writes)

    def xT_tokens(tt):
        return [("xT", 4 * tt + i, hf) for i in range(4) for hf in range(2)]

    evac_flip = [0]

    def transpose_tile(t, force=None):
        for hf in range(2):
            b = P.bank()

            def fn(e, t=t, hf=hf, b=b):
                ins = None
                for k in range(4):
                    kc = hf * 4 + k
                    ins = e.transpose(ps[b][:, k * 128:(k + 1) * 128], xres[:, t, kc * 128:(kc + 1) * 128], ident[:, :])
                return ins
            P.add("pe", fn, reads=[("xres", t), "ident"], writes=[("ps", b)])
            src = ps[b][:, :].rearrange("p (k t) -> p k t", k=4)
            dst = xT[:, hf * 4:(hf + 1) * 4, t * 128:(t + 1) * 128]
            evac_flip[0] ^= 1
            if force == "act" or (force is None and evac_flip[0]):
                P.add("act", lambda e, s=src, d=dst: e.activation(out=d, in_=s, func=AF.Copy),
                      reads=[("ps", b)], writes=[("xT", t, hf)])
            else:
                P.add("dve", lambda e, s=src, d=dst: e.tensor_copy(out=d, in_=s),
                      reads=[("ps", b)], writes=[("xT", t, hf)])

    ds_ln = [P.dsem(), P.dsem()]

    def load_ln(k):
        dma("sp", lngb[:, 0, :], lngb_d[2 * k:2 * k + 1, :].to_broadcast([128, D]), [("lng",)], ds_ln[0])
        dma("sp", lngb[:, 1, :], lngb_d[2 * k + 1:2 * k + 2, :].to_broadcast([128, D]), [("lnb",)], ds_ln[1])

    ds_out = P.dsem()

    def ln_a(t):
        s = t % 2

        def f1(e):
            e.bn_stats(out=bnst[:, s, 0:6], in_=xres[:, t, 0:512])
            return e.bn_stats(out=bnst[:, s, 6:12], in_=xres[:, t, 512:1024])
        P.add("dve", f1, reads=[("xres", t)], writes=[("bnst", s)])
        P.add("dve", lambda e: e.bn_aggr(out=mv[:, s, :], in_=bnst[:, s, :]), reads=[("bnst", s)], writes=[("mv", s)])
        P.add("act", lambda e: e.activation(out=stdt[:, s, :], in_=mv[:, s, 1:2], func=AF.Sqrt, bias=epsb[:, :], scale=1.0),
              reads=[("mv", s), "epsb"], writes=[("std", s)])

    def ln_b(t):
        s = t % 2
        xt = xres[:, t, :]
        P.add("dve", lambda e: e.reciprocal(out=rstd[:, s, :], in_=stdt[:, s, :]), reads=[("std", s)], writes=[("rstd", s)])
        P.add("dve", lambda e: e.tensor_scalar(out=nmr[:, s, :], in0=mv[:, s, 0:1], scalar1=rstd[:, s, :], scalar2=-1.0,
                                               op0=ALU.mult, op1=ALU.mult),
              reads=[("mv", s), ("rstd", s)], writes=[("nmr", s)])
        P.add("act", lambda e: e.activation(out=xt, in_=xt, func=AF.Identity, bias=nmr[:, s, :], scale=rstd[:, s, :]),
              reads=[("xres", t), ("rstd", s), ("nmr", s)], writes=[("xres", t)])

    def ln_c1(t, final=False):
        xt = xres[:, t, :]
        P.add("dve", lambda e: e.tensor_tensor(out=xt, in0=xt, in1=lngb[:, 0, :], op=ALU.mult),
              reads=[("xres", t), ("lng",)], writes=[("xres", t)])
        P.add("pool" if final else "dve", lambda e: e.tensor_tensor(out=xt, in0=xt, in1=lngb[:, 1, :], op=ALU.add),
              reads=[("xres", t), ("lnb",)], writes=[("xres", t)])

    def ln_c2(t, final):
        if final:
            dma("sp", y_d[t * 128:(t + 1) * 128, :], xres[:, t, :], [("out", t)], ds_out, reads=[("xres", t)])
        else:
            transpose_tile(t, force="act")

    def ln_step(t, t0, t1, final):
        if t - 3 >= t0:
            ln_c2(t - 3, final)
        ln_a(t)
        if t - 1 >= t0:
            ln_b(t - 1)
        if t - 2 >= t0:
            ln_c1(t - 2, final)
        if t == t1:
            ln_b(t)
            if t - 1 >= t0:
                ln_c1(t - 1, final)
            ln_c1(t, final)
            for k in (t - 2, t - 1, t):
                if k >= t0:
                    ln_c2(k, final)

    epsb = sb("epsb", [128, 1], F32)
    ds_c0 = P.dsem()
    dma("sp", ident[:, :], ident_d[:, :], ["ident"], ds_c0)
    P.add("dve", lambda e: e.memset(epsb[:, :], EPS2), writes=["epsb"])

    ds_x = [P.dsem() for _ in range(NT)]
    for t in range(NT):
        dma("sp", xres[:, t, :], x_d[t * 128:(t + 1) * 128, :], [("xres", t)], ds_x[t])
    xt_done = set()

    def need_xT(tt):
        if tt not in xt_done:
            xt_done.add(tt)
            for t in range(4 * tt, 4 * tt + 4):
                transpose_tile(t)

    need_xT(0)

    def ffn(f, ln_k, final, gate_tok):
        ds_w1 = [P.dsem() for _ in range(3)]
        ds_w2 = [P.dsem() for _ in range(2)]
        load_ln(ln_k)
        w1_issued = [0]

        def issue_w1(i):
            if i < NCH and i == w1_issued[0]:
                s = i % 3
                cast_dma(w1s[s], w1_d[f][i], [("w1", s)], ds_w1[s])
                w1_issued[0] += 1

        issue_w1(0)
        issue_w1(1)
        slc = [0]
        for gi, (c0, G) in enumerate(GROUPS):
            ws = gi % 2
            cast_dma(w2s[ws][:, 0:G, :], w2_d[f][c0:c0 + G].rearrange("g p n -> p g n"), [("w2", ws)], ds_w2[ws])
            for il in range(G):
                i = c0 + il
                issue_w1(i + 2)
                s = i % 3
                for tt in range(4):
                    if f == 0:
                        need_xT(tt)
                    bA = P.bank()
                    bU = P.bank()
                    rhs = [xT[:, kc, tt * 512:(tt + 1) * 512] for kc in range(8)]
                    mm(ps[bA][:, :], [(w1s[s][:, kc, 0:128], rhs[kc]) for kc in range(8)],
                       [("w1", s)] + xT_tokens(tt), [("ps", bA)])
                    mm(ps[bU][:, :], [(w1s[s][:, kc, 128:256], rhs[kc]) for kc in range(8)],
                       [("w1", s)] + xT_tokens(tt), [("ps", bU)])
                    k = slc[0] % 2
                    slc[0] += 1
                    P.add("act", lambda e, k=k, bA=bA: e.activation(out=sls[k], in_=ps[bA][:, :], func=AF.Silu),
                          reads=[("ps", bA)], writes=[("sl", k)])
                    P.add("dve", lambda e, k=k, bU=bU, il=il, tt=tt: e.tensor_tensor(
                        out=gT[:, il, tt * 512:(tt + 1) * 512], in0=sls[k], in1=ps[bU][:, :], op=ALU.mult),
                        reads=[("sl", k), ("ps", bU)], writes=[("gT", il, tt)])
            last = gi == len(GROUPS) - 1
            for t in range(NT):
                for hf in range(2):
                    b = P.bank()
                    mm(ps[b][:, :], [(gT[:, il, t * 128:(t + 1) * 128], w2s[ws][:, il, hf * 512:(hf + 1) * 512]) for il in range(G)],
                       [("gT", il, t // 4) for il in range(G)] + [("w2", ws)], [("ps", b)])
                    wr = [("xres", t)] + ([gate_tok] if last else [])
                    P.add("dve", lambda e, b=b, t=t, hf=hf: e.scalar_tensor_tensor(
                        out=xres[:, t, hf * 512:(hf + 1) * 512], in0=ps[b][:, :], scalar=C_FFN,
                        in1=xres[:, t, hf * 512:(hf + 1) * 512], op0=ALU.mult, op1=ALU.add),
                        reads=[("ps", b), ("xres", t)], writes=wr)
                if last:
                    ln_step(t, 0, NT - 1, final)

    ffn(0, 0, stage == 1, "GATE1")


    swam_f = swam.rearrange("p a b c -> p (a b c)")
    tri_f = tri.rearrange("p a b -> p (a b)")
    gate_f = gate_sb.rearrange("p a b -> p (a b)")
    rot = {"pt": 0, "e32": 0, "ev": 0, "sg": 0, "acc": 0, "on": 0}

    def evac(dst, src, reads, writes, eng=None):
        if eng is None:
            rot["ev"] ^= 1
            eng = "act" if rot["ev"] else "dve"
        if eng == "act":
            P.add("act", lambda e: e.activation(out=dst, in_=src, func=AF.Copy), reads=reads, writes=writes)
        else:
            P.add("dve", lambda e: e.tensor_copy(out=dst, in_=src), reads=reads, writes=writes)

    def v4(ap):
        return ap.rearrange("p (h q) -> p h q", h=4)

    def mixer(final):
        P.default_reads = ["GATE1"]
        ds_c = P.dsem()
        ds_cp = [P.dsem() for _ in range(4)]
        ds_wq = [P.dsem(), P.dsem()]
        ds_wv = P.dsem()
        ds_qa = [P.dsem(), P.dsem()]
        ds_os = [P.dsem(), P.dsem()]
        load_ln(1)
        cast_dma(wvs, wv_d[:, :, :], [("wv",)], ds_wv)
        cast_dma(wqs[0], wqk_d[UNIT_ORDER[0][0] * 2 + UNIT_ORDER[0][1]], [("wq", 0)], ds_wq[0])
        cast_dma(kbufs[0][64:79, :], kaug_d[:, :], [("kTaug", 0)], ds_cp[0])
        cast_dma(swam_f, swam_d[:, :], ["swam"], ds_cp[1])
        cast_dma(tri_f, tri_d[:, :], ["tri"], ds_cp[2])
        dma("sp", expsink[:, :], sinks_d[0:1, :].to_broadcast([128, 8]), ["sinkraw"], ds_c)
        P.add("act", lambda e: e.activation(out=expsink[:, :], in_=expsink[:, :], func=AF.Exp),
              reads=["sinkraw"], writes=["expsink"])

        def fc(e):
            e.memset(selbpad, 0.0)
            e.memset(VA[:, :, :, 64:65], 1.0)
            return e.memset(VB[:, :, :, 64:65], 1.0)
        P.add("dve", fc, writes=["selb", "vconst"])

        for t in range(NT):
            b = P.bank()
            mm(ps[b][:, 0:256], [(xT[:, kc, t * 128:(t + 1) * 128], wvs[:, kc, :]) for kc in range(8)],
               [("wv",), ("xT", t, 0), ("xT", t, 1)], [("ps", b)])
            ve = "act" if t % 2 == 0 else "dve"
            ex = ["vdone"] if t == NT - 1 else []
            evac(VA[:, t, :, 0:64], ps[b][:, 0:128].rearrange("p (g d) -> p g d", g=2), [("ps", b)], [("VA", t)] + ex, eng=ve)
            evac(VB[:, t, :, 0:64], ps[b][:, 128:256].rearrange("p (g d) -> p g d", g=2), [("ps", b)], [("VB", t)] + ex, eng=ve)
        cast_dma(kbufs[1][64:79, :], kaug_d[:, :], [("kTaug", 1)], ds_cp[3], reads=["vdone"])

        def proj_thunks(br, g, bi, ev):
            s = bi
            qTf, kT = qbufs[bi], kbufs[bi]
            th = []
            if br == 1:
                def t0():
                    cast_dma(qTf[72:79, :, :], qaug_d[g].rearrange("r (c n) -> r c n", c=16), [("qalibi", bi)], ds_qa[bi])
                    P.add("dve", lambda e: e.memset(qTf[64:72, 0:8, :], 0.0), writes=[("qaug0", bi)])
                th.append(t0)
            for u in (4, 0, 1, 2, 3):
                for tt in range(4):
                    def tk(u=u, tt=tt):
                        b = P.bank()
                        mm(ps[b][0:64, :], [(wqs[s][:, kc, u * 64:(u + 1) * 64], xT[:, kc, tt * 512:(tt + 1) * 512]) for kc in range(8)],
                           [("wq", s)] + xT_tokens(tt), [("ps", b)])
                        if u < 4:
                            evac(qTf[0:64, 4 * tt:4 * tt + 4, u * 128:(u + 1) * 128],
                                 ps[b][0:64, :].rearrange("p (c q) -> p c q", c=4), [("ps", b)], [("qT", bi, tt, u)], eng=ev)
                        else:
                            kex = ["vdone"] if bi == 1 else []
                            evac(kT[0:64, tt * 512:(tt + 1) * 512], ps[b][0:64, :], [("ps", b)] + kex, [("kT", bi, tt)], eng="dve")
                            if br == 1:
                                P.add("dve", lambda e: e.tensor_reduce(
                                    out=ksums[g][0:64, 2 * tt:2 * tt + 2], in_=ps[b][0:64, :].rearrange("p (a b) -> p a b", a=2),
                                    axis=AX.X, op=ALU.add), reads=[("ps", b)], writes=[("ksum", g, tt)])
                                if tt == 3:
                                    P.add("dve", lambda e: e.tensor_scalar(out=kmeans[g][0:64, :], in0=ksums[g][0:64, :],
                                                                           scalar1=1.0 / 256.0, scalar2=None, op0=ALU.mult),
                                          reads=[("ksum", g, t2) for t2 in range(4)], writes=[("kmean", g)])
                    th.append(tk)
            return th

        def drain(th, n):
            for _ in range(min(n, len(th))):
                th.pop(0)()

        def finish_chunk(k, oT, g, c, tokname, eng):
            bt = P.bank()

            def ft(e):
                ins = None
                for k2 in range(2):
                    ins = e.transpose(ps[bt][:, k2 * 128:(k2 + 1) * 128], ons[k][:, k2 * 128:(k2 + 1) * 128], ident[:, :])
                return ins
            P.add("pe", ft, reads=[("on", k), "ident"], writes=[("ps", bt)])
            evac(oT[:, 2 * g:2 * g + 2, c * 128:(c + 1) * 128], ps[bt][:, 0:256].rearrange("p (a q) -> p a q", a=2),
                 [("ps", bt)], [(tokname, g, c), "GATE2"], eng=eng)

        def swa(g, bi, nxt):
            qTf, kT = qbufs[bi], kbufs[bi]
            stA = {}

            def qtoks(c):
                return [("qT", bi, c // 4, u) for u in range(4)]

            def stageA(nb):
                js = [nb] if nb == 0 else [nb - 1, nb]
                pts = []
                for j in js:
                    w = 0 if j == nb else 1
                    bs = P.bank()
                    mm(ps[bs][:, :], [(kT[0:64, j * 128:(j + 1) * 128], qTf[0:64, nb, :])],
                       [("kT", bi, j // 4)] + qtoks(nb), [("ps", bs)])
                    pi = rot["pt"] % 4
                    rot["pt"] += 1
                    P.add("act", lambda e, bs=bs, pi=pi: e.activation(out=PTs[pi], in_=ps[bs][:, :], func=AF.Exp, scale=0.125),
                          reads=[("ps", bs)], writes=[("PT", pi)])
                    P.add("dve", lambda e, pi=pi, w=w: e.tensor_tensor(
                        out=v4(PTs[pi]), in0=v4(PTs[pi]), in1=swam[:, 4 * g:4 * g + 4, w, :], op=ALU.mult),
                        reads=[("PT", pi), "swam"], writes=[("PT", pi)])
                    pts.append((j, pi))
                stA[nb] = pts

            def stageB(nb):
                pts = stA.pop(nb)
                bo = 4 + rot["acc"] % 2
                rot["acc"] += 1

                def fpv(e):
                    ins = None
                    for hl in range(4):
                        for k, (j, pi) in enumerate(pts):
                            ins = e.matmul(ps[bo][:, hl * 65:(hl + 1) * 65], lhsT=PTs[pi][:, hl * 128:(hl + 1) * 128],
                                           rhs=VA[:, j, g, 0:65], start=(k == 0), stop=(k == len(pts) - 1))
                    return ins
                P.add("pe", fpv, reads=[("PT", pi) for _, pi in pts] + [("VA", j) for j, _ in pts] + ["vconst"],
                      writes=[("ps", bo)])
                k = rot["on"] % 2
                rot["on"] += 1
                P.add("dve", lambda e: e.tensor_tensor(
                    out=dsum.rearrange("p (h o) -> p h o", o=1),
                    in0=ps[bo][:, 0:260].rearrange("p (h d) -> p h d", h=4)[:, :, 64:65],
                    in1=expsink[:, 4 * g:4 * g + 4].rearrange("p (h o) -> p h o", o=1), op=ALU.add),
                    reads=[("ps", bo), "expsink"], writes=["dsum"])
                P.add("dve", lambda e: e.reciprocal(out=rdn, in_=dsum), reads=["dsum"], writes=["rd"])

                def fn(e):
                    ins = None
                    for hl in range(4):
                        ins = e.tensor_scalar(out=ons[k][:, hl * 64:(hl + 1) * 64], in0=ps[bo][:, hl * 65:hl * 65 + 64],
                                              scalar1=rdn[:, hl:hl + 1], scalar2=None, op0=ALU.mult)
                    return ins
                P.add("dve", fn, reads=[("ps", bo), "rd"], writes=[("on", k)])
                return (k, oTA, g, nb, "oTA", "act")

            pendC = []
            per = -(-len(nxt) // NT)
            for step in range(NT + 2):
                if step < NT:
                    stageA(step)
                if 0 <= step - 1 < NT:
                    pendC.append(stageB(step - 1))
                if 0 <= step - 2 < NT:
                    finish_chunk(*pendC.pop(0))
                drain(nxt, per)
            drain(nxt, len(nxt))

        def moba(g, bi, nxt):
            qTf, kT = qbufs[bi], kbufs[bi]
            kmean = kmeans[g]

            def qtoks(c):
                return [("qT", bi, c // 4, u) for u in range(4)]
            P.add("dve", lambda e: e.memset(gate_f, -1.0e30), writes=["gate_sb"])

            def part1(c):
                qb = c // 2
                bg = P.bank()

                def fg(e):
                    ins = None
                    for hl in range(4):
                        ins = e.matmul(ps[bg][:, hl * 8:hl * 8 + qb], lhsT=qTf[0:64, c, hl * 128:(hl + 1) * 128],
                                       rhs=kmean[0:64, 0:qb], start=True, stop=True)
                    return ins
                P.add("pe", fg, reads=qtoks(c) + [("kmean", g)], writes=[("ps", bg)])
                P.add("dve", lambda e: e.tensor_copy(out=gate_sb[:, :, 0:qb],
                                                     in_=ps[bg][:, 0:32].rearrange("p (h n) -> p h n", h=4)[:, :, 0:qb]),
                      reads=[("ps", bg), "gate_sb"], writes=["gate_sb"])

                def fm(e):
                    ins = None
                    for hl in range(4):
                        ins = e.max(out=top8[:, hl, :], in_=gate_sb[:, hl, :])
                    return ins
                P.add("dve", fm, reads=["gate_sb"], writes=["top8"])

                def fs(e):
                    ins = None
                    for hl in range(4):
                        ins = e.tensor_scalar(out=selbpad[:, 64 + 8 * hl:72 + 8 * hl], in0=gate_sb[:, hl, :],
                                              scalar1=top8[:, hl, 2:3], scalar2=NEG_SEL, op0=ALU.is_lt, op1=ALU.mult)
                    return ins
                P.add("dve", fs, reads=["gate_sb", "top8"], writes=["selb"])
                P.add("dve", lambda e: e.memset(selbpad[:, 64:96].rearrange("p (h n) -> p h n", h=4)[:, :, qb:qb + 1], 0.0),
                      writes=["selb"])

            def part2(c):
                bt = P.bank()

                def ft(e):
                    ins = None
                    for hl in range(4):
                        ins = e.transpose(ps[bt][0:72, hl * 128:(hl + 1) * 128], selbpad[:, 8 * hl:8 * hl + 72], ident[:, :])
                    return ins
                P.add("pe", ft, reads=["selb", "ident"], writes=[("ps", bt)])
                P.add("dve", lambda e: e.tensor_copy(out=qTf[64:72, c, :], in_=ps[bt][64:72, :]),
                      reads=[("ps", bt)], writes=[("qaug", bi, c)])

            prev = []

            def attn(c):
                js = list(range(c + 1))
                qa = ("qaug", bi, c) if c >= 8 else ("qaug0", bi)
                pend = []

                def emit_pv():
                    idx, j, pi = pend.pop()

                    def fpv(e):
                        ins = None
                        for hl in range(4):
                            ins = e.matmul(ps[4 + hl][:, 0:65], lhsT=PTs[pi][:, hl * 128:(hl + 1) * 128], rhs=VB[:, j, g, 0:65],
                                           start=(idx == 0), stop=(idx == len(js) - 1))
                        return ins
                    P.add("pe", fpv, reads=[("PT", pi), ("VB", j), "vconst"], writes=[("ps", 4 + hl) for hl in range(4)])
                LOOK = 2
                n = len(js)
                for idx in range(n + LOOK):
                    if idx < n:
                        j = js[idx]
                        bs = P.bank()
                        mm(ps[bs][:, :], [(kT[0:79, j * 128:(j + 1) * 128], qTf[0:79, c, :])],
                           [("kT", bi, j // 4), ("kTaug", bi), ("qalibi", bi), qa] + qtoks(c), [("ps", bs)])
                        pi = rot["pt"] % 4
                        rot["pt"] += 1
                        P.add("act", lambda e, bs=bs, pi=pi: e.activation(out=PTs[pi], in_=ps[bs][:, :], func=AF.Exp, scale=0.125),
                              reads=[("ps", bs)], writes=[("PT", pi)])
                        if j == c:
                            P.add("dve", lambda e, pi=pi: e.tensor_tensor(out=v4(PTs[pi]), in0=v4(PTs[pi]), in1=tri, op=ALU.mult),
                                  reads=[("PT", pi), "tri"], writes=[("PT", pi)])
                        pend.insert(0, (idx, j, pi))
                    if idx == 1 and prev:
                        finish_chunk(*prev.pop())
                    if idx - LOOK >= 0:
                        emit_pv()
                if prev:
                    finish_chunk(*prev.pop())
                k = rot["on"] % 2
                rot["on"] += 1

                def fr(e):
                    ins = None
                    for hl in range(4):
                        ins = e.reciprocal(out=rdn[:, hl:hl + 1], in_=ps[4 + hl][:, 64:65])
                    return ins
                P.add("dve", fr, reads=[("ps", 4 + hl) for hl in range(4)], writes=["rd"])

                def fn(e):
                    ins = None
                    for hl in range(4):
                        ins = e.tensor_scalar(out=ons[k][:, hl * 64:(hl + 1) * 64], in0=ps[4 + hl][:, 0:64],
                                              scalar1=rdn[:, hl:hl + 1], scalar2=None, op0=ALU.mult)
                    return ins
                P.add("dve", fn, reads=[("ps", 4 + hl) for hl in range(4)] + ["rd"], writes=[("on", k)])
                prev.append((k, oTB, g, c, "oTB", "dve"))

            tot = sum(c + 1 for c in range(NT))
            done = 0
            n0 = len(nxt)
            for c in range(NT):
                if 8 <= c + 1 < NT:
                    part1(c + 1)
                attn(c)
                if 8 <= c + 1 < NT:
                    part2(c + 1)
                done += c + 1
                want = -(-n0 * done // tot) if c >= 2 else 0
                drain(nxt, max(0, want - (n0 - len(nxt))))
            finish_chunk(*prev.pop())
            drain(nxt, len(nxt))

        P.bank_pool = list(range(4))
        order = UNIT_ORDER
        th = proj_thunks(order[0][0], order[0][1], 0, None)
        drain(th, len(th))
        for k, (br, g) in enumerate(order):
            bi = k % 2
            nxt = []
            if k + 1 < 4:
                nb_, ng_ = order[k + 1]
                s = (k + 1) % 2
                cast_dma(wqs[s], wqk_d[nb_ * 2 + ng_], [("wq", s)], ds_wq[s])
                nxt = proj_thunks(nb_, ng_, s, "act" if br == 0 else "dve")
            if k == 3:
                cast_dma(oss[0], outw_d[0], [("os", 0), ("wq", 0), ("wq", 1)], ds_os[0])
            if br == 0:
                swa(g, bi, nxt)
            else:
                moba(g, bi, nxt)
        P.bank_pool = list(range(8))

        P.default_reads = ["GATE2"]
        ds_wo = P.dsem()
        cnt = 0
        for th in range(2):
            for dc in range(8):
                s = cnt % 2
                cnt += 1
                if cnt > 1:
                    cast_dma(oss[s], outw_d[dc], [("os", s)], ds_os[s])
                if cnt == 2:
                    cast_dma(wos, wo_d[:, :, :], ["wo"], ds_wo)
                for tl in range(2):
                    tt = 2 * th + tl
                    bga, bgb, bya, byb = P.bank(), P.bank(), P.bank(), P.bank()
                    tok = slice(tt * 512, (tt + 1) * 512)
                    mm(ps[bga][:, :], [(oss[s][:, kc * 128:(kc + 1) * 128], xT[:, kc, tok]) for kc in range(8)],
                       [("os", s)] + xT_tokens(tt), [("ps", bga)])
                    mm(ps[bgb][:, :], [(oss[s][:, 1024 + kc * 128:1024 + (kc + 1) * 128], xT[:, kc, tok]) for kc in range(8)],
                       [("os", s)] + xT_tokens(tt), [("ps", bgb)])
                    mm(ps[bya][:, :], [(oss[s][:, 2048 + hc * 128:2048 + (hc + 1) * 128], oTA[:, hc, tok]) for hc in range(4)],
                       [("os", s)] + [("oTA", g, nb) for g in range(2) for nb in range(4 * tt, 4 * tt + 4)], [("ps", bya)])
                    mm(ps[byb][:, :], [(oss[s][:, 2560 + hc * 128:2560 + (hc + 1) * 128], oTB[:, hc, tok]) for hc in range(4)],
                       [("os", s)] + [("oTB", g, nb) for g in range(2) for nb in range(4 * tt, 4 * tt + 4)], [("ps", byb)])
                    k = rot["sg"] % 2
                    rot["sg"] += 1
                    P.add("act", lambda e, k=k, b=bga: e.activation(out=sAs[k], in_=ps[b][:, :], func=AF.Sigmoid),
                          reads=[("ps", bga)], writes=[("sA", k)])
                    P.add("act", lambda e, k=k, b=bgb: e.activation(out=sBs[k], in_=ps[b][:, :], func=AF.Sigmoid),
                          reads=[("ps", bgb)], writes=[("sB", k)])
                    P.add("dve", lambda e, k=k, b=bya: e.tensor_tensor(out=t1s, in0=sAs[k], in1=ps[b][:, :], op=ALU.mult),
                          reads=[("sA", k), ("ps", bya)], writes=["t1"])
                    P.add("dve", lambda e, k=k, b=byb: e.tensor_tensor(out=t2s, in0=sBs[k], in1=ps[b][:, :], op=ALU.mult),
                          reads=[("sB", k), ("ps", byb)], writes=["t2"])
                    P.add("dve", lambda e, dc=dc, tl=tl: e.tensor_tensor(out=yT[:, dc, tl * 512:(tl + 1) * 512], in0=t1s, in1=t2s,
                                                                        op=ALU.add),
                          reads=["t1", "t2"], writes=[("yT", dc, tl)])
            for t4 in range(8):
                t = 8 * th + t4
                for hf in range(2):
                    b = P.bank()
                    mm(ps[b][:, :], [(yT[:, dc, t4 * 128:(t4 + 1) * 128], wos[:, dc, hf * 512:(hf + 1) * 512]) for dc in range(8)],
                       [("yT", dc, t4 // 4) for dc in range(8)] + ["wo"], [("ps", b)])
                    wr = [("xres", t)] + (["GATE3"] if th == 1 else [])
                    P.add("dve", lambda e, b=b, t=t, hf=hf: e.scalar_tensor_tensor(
                        out=xres[:, t, hf * 512:(hf + 1) * 512], in0=ps[b][:, :], scalar=C_MIX,
                        in1=xres[:, t, hf * 512:(hf + 1) * 512], op0=ALU.mult, op1=ALU.add),
                        reads=[("ps", b), ("xres", t)], writes=wr)
                ln_step(t, 8 * th, 8 * th + 7, final)

    if stage >= 2:
        mixer(stage == 2)
    if stage >= 3:
        P.default_reads = ["GATE3"]
        ffn(1, 2, True, "GATE4")

    P.add("sp", lambda e: e.nop(), reads=[("out", t) for t in range(NT)])
    P.emit()


def _prep_ffn(w_in, w_out):
    a = w_in[:, :DFF].reshape(8, 128, NCH, 128)
    u = w_in[:, DFF:].reshape(8, 128, NCH, 128)
    w1 = np.concatenate([a, u], axis=3).transpose(2, 1, 0, 3)
    return np.ascontiguousarray(w1, dtype=np.float32), np.ascontiguousarray(w_out.reshape(NCH, 128, D), dtype=np.float32)


def _consts():
    i = np.arange(1, 17, dtype=np.float64)
    slopes = np.exp2(-8.0 * i / 16.0)
    sa, sbb = slopes[:8], slopes[8:]
    kj = np.arange(128)[:, None]
    qi = np.arange(128)[None, :]
    swam = np.zeros((128, 8, 2, 128), np.float64)
    d0 = (qi - kj).astype(np.float64)
    d1 = (qi + 128 - kj).astype(np.float64)
    for h in range(8):
        swam[:, h, 0, :] = np.where(d0 >= 0, np.exp(-sa[h] * d0), 0.0)
        swam[:, h, 1, :] = np.where(d1 < 128, np.exp(-sa[h] * d1), 0.0)
    p = np.arange(128, dtype=np.float64)[:, None, None]
    dl = np.arange(16, dtype=np.float64)[None, None, :]
    tb = -sbb[None, :, None] * (dl * 128.0 - p)
    tri = np.broadcast_to((kj <= qi)[:, None, :], (128, 4, 128)).astype(np.float32)
    onehot = (np.arange(S)[None, :] // 256 == np.arange(8)[:, None]).astype(np.float32)
    def bf(a):
        u = np.ascontiguousarray(np.asarray(a, np.float32)).view(np.uint32).astype(np.uint64)
        u = (u + 0x7FFF + ((u >> 16) & 1)) & 0xFFFF0000
        return u.astype(np.uint32).view(np.float32)
    keys = np.arange(S)
    kaug = np.zeros((15, S), np.float32)
    kaug[0:8] = onehot
    kaug[8:11] = keys // 128
    kaug[11:14] = keys % 128
    kaug[14] = 1.0
    sh1 = bf(sbb)
    sh2 = bf(sbb - sh1.astype(np.float64))
    sh3 = bf(sbb - sh1.astype(np.float64) - sh2.astype(np.float64))
    qaug = np.zeros((2, 7, 16, 4, 128), np.float32)
    for g in range(2):
        for hl in range(4):
            h = 4 * g + hl
            qaug[g, 0, :, hl, :] = 1024.0 * sh1[h]
            qaug[g, 1, :, hl, :] = 1024.0 * sh2[h]
            qaug[g, 2, :, hl, :] = 1024.0 * sh3[h]
            qaug[g, 3, :, hl, :] = 8.0 * sh1[h]
            qaug[g, 4, :, hl, :] = 8.0 * sh2[h]
            qaug[g, 5, :, hl, :] = 8.0 * sh3[h]
            qaug[g, 6, :, hl, :] = (-1024.0 * np.arange(16) * sbb[h])[:, None]
    return dict(
        ident=np.eye(128, dtype=np.float32),
        swam=np.ascontiguousarray(swam.reshape(128, -1), dtype=np.float32),
        tb=np.ascontiguousarray(tb.reshape(128, 128), dtype=np.float32),
        tri=np.ascontiguousarray(tri.reshape(128, 512)),
        kaug=np.ascontiguousarray(kaug),
        qaug=np.ascontiguousarray(qaug.reshape(2, 7, 16 * 512)),
    )


def _prep_mix(w_in, wa, wb, wo):
    W = w_in.reshape(8, 128, 3584)

    def cols(lo, n):
        return W[:, :, lo:lo + n]
    qA, kA, vA, qB, kB, vB, gA, gB = 0, 512, 640, 768, 1280, 1408, 1536, 2560
    wqk = np.empty((4, 128, 8, 320), np.float32)
    for br, (q0, k0) in enumerate([(qA, kA), (qB, kB)]):
        for g in range(2):
            blk = np.concatenate([cols(q0 + g * 256, 256), cols(k0 + g * 64, 64)], axis=2)
            wqk[br * 2 + g] = blk.transpose(1, 0, 2)
    wv = np.ascontiguousarray(np.concatenate([cols(vA, 128), cols(vB, 128)], axis=2).transpose(1, 0, 2))
    outw = np.empty((8, 128, 3072), np.float32)
    for dc in range(8):
        ga = cols(gA + dc * 128, 128).transpose(1, 0, 2).reshape(128, 1024)
        gb = cols(gB + dc * 128, 128).transpose(1, 0, 2).reshape(128, 1024)
        a = wa[:, dc * 128:(dc + 1) * 128].reshape(4, 128, 128).transpose(1, 0, 2).reshape(128, 512)
        b = wb[:, dc * 128:(dc + 1) * 128].reshape(4, 128, 128).transpose(1, 0, 2).reshape(128, 512)
        wab = np.concatenate([a, b], axis=1)
        outw[dc] = np.concatenate([ga, gb, wab], axis=1)
    wo_l = np.ascontiguousarray(wo.reshape(8, 128, 1024).transpose(1, 0, 2))
    return wqk, wv, outw, wo_l


def make_in_maps(inputs, n_cores=8):
    f32 = lambda a: np.asarray(a, dtype=np.float32)
    w1_0, w2_0 = _prep_ffn(f32(inputs["ffn1_w_in"])[0], f32(inputs["ffn1_w_out"])[0])
    w1_1, w2_1 = _prep_ffn(f32(inputs["ffn2_w_in"])[0], f32(inputs["ffn2_w_out"])[0])
    wqk, wv, outw, wo_l = _prep_mix(f32(inputs["mix_w_in"])[0], f32(inputs["w_branch_a"])[0],
                                    f32(inputs["w_branch_b"])[0], f32(inputs["mix_w_o"])[0])
    lngb = np.ascontiguousarray(np.stack([f32(inputs[k])[0] for k in
                                          ("ln1_g", "ln1_b", "ln2_g", "ln2_b", "ln3_g", "ln3_b")]))
    shared = dict(w1_0=w1_0, w2_0=w2_0, w1_1=w1_1, w2_1=w2_1, lngb=lngb, wqk=wqk, wv=wv, outw=outw, wo=wo_l,
                  sinks=np.ascontiguousarray(f32(inputs["swa_sinks"]).reshape(1, 8)))
    shared.update(_consts())
    x = f32(inputs["x"])
    maps = []
    for c in range(n_cores):
        m = dict(shared)
        m["x"] = np.ascontiguousarray(x[c])
        maps.append(m)
    return maps


def kernel(**inputs):
    nc = build(3)
    maps = make_in_maps(inputs, 8)
    res = run_bass_kernel_spmd(nc, maps, core_ids=list(range(8)))
    out = np.stack([np.asarray(r["y"], dtype=np.float32) for r in res.results], axis=0)
    return out
```
